# Optimizing a Trainium2 kernel written in Bass

```python
import math
import jax, jax.numpy as jnp
from jax import lax
import numpy as np

D_MODEL = 1024
BATCH = 4
SEQ = 8192
DEPTH = 1

MEM_LEN = 256
EPS = 1e-6
S5_WIDTH = 512
S5_GROUP = 16
S5_GROUPS = S5_WIDTH // S5_GROUP
S5_STATE = 64
SGU_WIDTH = 1024
SGU_HEADS = 8
SGU_HEAD_DIM = SGU_WIDTH // SGU_HEADS
CHUNK = 128
XATTN_HEADS = 4
XATTN_HEAD_DIM = D_MODEL // XATTN_HEADS
D_FF = ((8 * D_MODEL // 3 + 255) // 256) * 256
OFF_U = S5_WIDTH
OFF_V = OFF_U + SGU_WIDTH
OFF_GA = OFF_V + SGU_WIDTH
OFF_GB = OFF_GA + D_MODEL
IN_COLS = OFF_GB + D_MODEL

kernel_name = "hybrid_s5_gmlp_gated_encoder"


def rms_norm(x, g):
    xf = x.astype(jnp.float32)
    y = xf * lax.rsqrt(jnp.mean(xf * xf, axis=-1, keepdims=True) + EPS)
    return (y * g.astype(jnp.float32)).astype(x.dtype)


def layer_norm(x, g, b):
    xf = x.astype(jnp.float32)
    mu = jnp.mean(xf, axis=-1, keepdims=True)
    xc = xf - mu
    y = xc * lax.rsqrt(jnp.mean(xc * xc, axis=-1, keepdims=True) + EPS)
    return (y * g.astype(jnp.float32) + b.astype(jnp.float32)).astype(x.dtype)


def _linear_recurrence(e1, e2):
    a1, b1 = e1
    a2, b2 = e2
    return a1 * a2, a2 * b1 + b2


def s5_scan(u, lam_re, lam_im, log_step, b_re, b_im, c_re, c_im, reverse):
    f32 = jnp.float32
    seq = u.shape[0]
    lam = lax.complex(lam_re.astype(f32), lam_im.astype(f32))
    step = jnp.exp(log_step.astype(f32))[:, None]
    lam_bar = jnp.exp(lam * step)
    b = lax.complex(b_re.astype(f32), b_im.astype(f32))
    b_bar = ((lam_bar - 1.0) / lam)[..., None] * b
    bu = jnp.einsum('gpc,lbgc->lbgp', b_bar, u.astype(jnp.complex64))
    a = jnp.broadcast_to(lam_bar[None, None], (seq, 1) + lam_bar.shape)
    _, states = lax.associative_scan(_linear_recurrence, (a, bu), axis=0, reverse=reverse)
    c = lax.complex(c_re.astype(f32), c_im.astype(f32))
    return jnp.einsum('gcp,lbgp->lbgc', c, states).real


def s5_branch(xa, lam_re, lam_im, log_step, b_re, b_im, c_re, c_im, d, w_glu):
    f32 = jnp.float32
    bsz, seq, _ = xa.shape
    u = xa.astype(f32).reshape(bsz, seq, S5_GROUPS, S5_GROUP).transpose(1, 0, 2, 3)
    y = d.astype(f32).reshape(S5_GROUPS, S5_GROUP) * u
    for direction, reverse in ((0, False), (1, True)):
        y = y + s5_scan(u, lam_re[direction], lam_im[direction], log_step[direction],
                        b_re[direction], b_im[direction], c_re[direction], c_im[direction],
                        reverse)
    y = jax.nn.gelu(y.transpose(1, 0, 2, 3).reshape(bsz, seq, S5_WIDTH))
    y = y * jax.nn.sigmoid(y @ w_glu.astype(f32))
    return y.astype(xa.dtype)


def sgu_branch(zu, zv, ln_g, ln_b, w_s, bias):
    bsz, seq, _ = zu.shape
    zu = jax.nn.gelu(zu)
    zv = layer_norm(jax.nn.gelu(zv), ln_g, ln_b)
    zv = zv.reshape(bsz, seq // CHUNK, CHUNK, SGU_HEADS, SGU_HEAD_DIM)
    sv = jnp.einsum('hts,bnshd->bnthd', w_s, zv) + bias.T[None, None, :, :, None]
    return zu * sv.reshape(bsz, seq, SGU_WIDTH)


def memory_cross_attention(hn, memn, w_q, w_k, w_v, w_o):
    bsz, seq, _ = hn.shape
    q = (hn @ w_q).reshape(bsz, seq, XATTN_HEADS, XATTN_HEAD_DIM)
    k = (memn @ w_k).reshape(bsz, MEM_LEN, XATTN_HEADS, XATTN_HEAD_DIM)
    v = (memn @ w_v).reshape(bsz, MEM_LEN, XATTN_HEADS, XATTN_HEAD_DIM)
    s = jnp.einsum('blhd,bmhd->bhlm', q, k).astype(jnp.float32) * (XATTN_HEAD_DIM ** -0.5)
    p = jax.nn.softmax(s, axis=-1).astype(v.dtype)
    o = jnp.einsum('bhlm,bmhd->blhd', p, v).reshape(bsz, seq, D_MODEL)
    return o @ w_o


def swiglu(hn, w_gate, w_up, w_down):
    return (jax.nn.silu(hn @ w_gate) * (hn @ w_up)) @ w_down


def setup_inputs(seed: int = 0) -> dict:
    key = jax.random.key(seed)
    ks = iter(jax.random.split(key, 40))
    f32 = jnp.float32

    def nrm(shape, scale):
        return jax.random.normal(next(ks), shape, f32) * scale

    def gain(shape):
        return 1.0 + nrm(shape, 0.02)

    L, G, P, C, H = DEPTH, S5_GROUPS, S5_STATE, S5_GROUP, SGU_HEADS
    n_idx = jnp.arange(P, dtype=f32)
    lam_re = -0.5 + nrm((L, 2, G, P), 0.01)
    lam_im = math.pi * n_idx + nrm((L, 2, G, P), 0.01)
    log_step = jax.random.uniform(next(ks), (L, 2, G), f32, math.log(1e-3), math.log(1e-1))
    return {
        "x": nrm((BATCH, SEQ, D_MODEL), 1.0),
        "mem": nrm((BATCH, MEM_LEN, D_MODEL), 1.0),
        "mix_norm_g": gain((L, D_MODEL)),
        "w_in": nrm((L, D_MODEL, IN_COLS), D_MODEL ** -0.5),
        "s5_lam_re": lam_re,
        "s5_lam_im": lam_im,
        "s5_log_step": log_step,
        "s5_b_re": nrm((L, 2, G, P, C), (2 * C) ** -0.5),
        "s5_b_im": nrm((L, 2, G, P, C), (2 * C) ** -0.5),
        "s5_c_re": nrm((L, 2, G, C, P), (2 * P) ** -0.5),
        "s5_c_im": nrm((L, 2, G, C, P), (2 * P) ** -0.5),
        "s5_d": nrm((L, S5_WIDTH), 1.0),
        "s5_w_glu": nrm((L, S5_WIDTH, S5_WIDTH), S5_WIDTH ** -0.5),
        "sgu_ln_g": gain((L, SGU_WIDTH)),
        "sgu_ln_b": nrm((L, SGU_WIDTH), 0.02),
        "sgu_w": nrm((L, H, CHUNK, CHUNK), 0.5 * CHUNK ** -0.5),
        "sgu_bias": 1.0 + nrm((L, H, CHUNK), 0.02),
        "w_proj_a": nrm((L, S5_WIDTH, D_MODEL), S5_WIDTH ** -0.5),
        "w_proj_b": nrm((L, SGU_WIDTH, D_MODEL), SGU_WIDTH ** -0.5),
        "w_out": nrm((L, D_MODEL, D_MODEL), D_MODEL ** -0.5),
        "xattn_norm_g": gain((L, D_MODEL)),
        "mem_norm_g": gain((D_MODEL,)),
        "w_q": nrm((L, D_MODEL, D_MODEL), D_MODEL ** -0.5),
        "w_k": nrm((L, D_MODEL, D_MODEL), D_MODEL ** -0.5),
        "w_v": nrm((L, D_MODEL, D_MODEL), D_MODEL ** -0.5),
        "w_xo": nrm((L, D_MODEL, D_MODEL), D_MODEL ** -0.5),
        "ffn_norm_g": gain((L, D_MODEL)),
        "w_gate": nrm((L, D_MODEL, D_FF), D_MODEL ** -0.5),
        "w_up": nrm((L, D_MODEL, D_FF), D_MODEL ** -0.5),
        "w_down": nrm((L, D_FF, D_MODEL), D_FF ** -0.5),
        "final_norm_g": gain((D_MODEL,)),
    }


def reference(x, mem, mix_norm_g, w_in, s5_lam_re, s5_lam_im, s5_log_step, s5_b_re, s5_b_im,
              s5_c_re, s5_c_im, s5_d, s5_w_glu, sgu_ln_g, sgu_ln_b, sgu_w, sgu_bias,
              w_proj_a, w_proj_b, w_out, xattn_norm_g, mem_norm_g, w_q, w_k, w_v, w_xo,
              ffn_norm_g, w_gate, w_up, w_down, final_norm_g):
    memn = rms_norm(mem, mem_norm_g)
    h = x
    for i in range(DEPTH):
        n = rms_norm(h, mix_norm_g[i])
        proj = n @ w_in[i]
        xa = proj[..., :OFF_U]
        zu = proj[..., OFF_U:OFF_V]
        zv = proj[..., OFF_V:OFF_GA]
        gate_a = jax.nn.sigmoid(proj[..., OFF_GA:OFF_GB])
        gate_b = jax.nn.sigmoid(proj[..., OFF_GB:])
        ya = s5_branch(xa, s5_lam_re[i], s5_lam_im[i], s5_log_step[i], s5_b_re[i], s5_b_im[i],
                       s5_c_re[i], s5_c_im[i], s5_d[i], s5_w_glu[i])
        yb = sgu_branch(zu, zv, sgu_ln_g[i], sgu_ln_b[i], sgu_w[i], sgu_bias[i])
        merged = gate_a * (ya @ w_proj_a[i]) + gate_b * (yb @ w_proj_b[i])
        h = h + merged @ w_out[i]
        h = h + memory_cross_attention(rms_norm(h, xattn_norm_g[i]), memn,
                                       w_q[i], w_k[i], w_v[i], w_xo[i])
        h = h + swiglu(rms_norm(h, ffn_norm_g[i]), w_gate[i], w_up[i], w_down[i])
    return rms_norm(h, final_norm_g)
```

```python
import numpy as np
from contextlib import ExitStack
import concourse.bass as bass
import concourse.mybir as mybir
from concourse.bass_utils import run_bass_kernel_spmd

F32 = mybir.dt.float32
BF16 = mybir.dt.bfloat16
AF = mybir.ActivationFunctionType
ALU = mybir.AluOpType

ENG = ['pe', 'act', 'dve', 'pool', 'sp']
NTOK = 4096
D = 1024
DFF = 2816
EPS = 1e-6
PI = float(np.pi)


class Res:
    __slots__ = ('name', 'last_w', 'readers', 'extra_w')

    def __init__(self, name=''):
        self.name = name
        self.last_w = None
        self.readers = {}
        self.extra_w = []


class Chan:
    def __init__(self, name, group=False):
        self.name = name
        self.group = group
        self.count = 0
        self.sem = None


class Ins:
    __slots__ = ('eng', 'fn', 'deps', 'signal', 'signo', 'chan', 'cval', 'idx')

    def __init__(self, eng, fn):
        self.eng = eng
        self.fn = fn
        self.deps = []
        self.signal = False
        self.signo = None
        self.chan = None
        self.cval = None


class Sched:
    def __init__(self, nc):
        self.nc = nc
        self.q = {e: [] for e in ENG}
        self.chans = []
        self.n = 0

    def chan(self, name, group=False):
        c = Chan(name, group)
        self.chans.append(c)
        return c

    def op(self, eng, fn, r=(), w=(), chan=None, extra=(), soft_w=()):
        ins = Ins(eng, fn)
        ins.idx = self.n
        self.n += 1
        deps = {}
        for res in r:
            if res.last_w is not None:
                deps[id(res.last_w)] = res.last_w
            for x in res.extra_w:
                deps[id(x)] = x
        for res in w:
            if res.last_w is not None:
                deps[id(res.last_w)] = res.last_w
            for rd in res.readers.values():
                deps[id(rd)] = rd
            for x in res.extra_w:
                deps[id(x)] = x
        for res in soft_w:
            if res.last_w is not None:
                deps[id(res.last_w)] = res.last_w
            for rd in res.readers.values():
                deps[id(rd)] = rd
            res.extra_w.append(ins)
        for d in extra:
            deps[id(d)] = d
        if chan is not None:
            ins.chan = chan
            chan.count += 1
            ins.cval = chan.count
        for d in deps.values():
            if d is ins:
                continue
            if d.chan is not None:
                ins.deps.append(d)
            elif d.eng != eng or eng != 'pe':
                if d.fn is None:
                    continue
                d.signal = True
                ins.deps.append(d)
        for res in r:
            key = eng if chan is None else ('dma', ins.idx)
            res.readers[key] = ins
        for res in w:
            res.last_w = ins
            res.readers = {}
        self.q[eng].append(ins)
        return ins

    def barrier(self):
        lasts = []
        for e in ENG:
            for ins in reversed(self.q[e]):
                if ins.chan is None and ins.fn is not None:
                    lasts.append(ins)
                    break
        dmas = {}
        for e in ENG:
            for ins in self.q[e]:
                if ins.chan is not None:
                    dmas[id(ins.chan)] = ins
        for e in ENG:
            self.op(e, None, extra=lasts + list(dmas.values()))

    def emit(self, final_waits=()):
        nc = self.nc
        with ExitStack() as st:
            esem = {}
            for e in ['pe', 'act', 'dve', 'pool']:
                esem[e] = st.enter_context(nc.semaphore('s_' + e))
            for c in self.chans:
                c.sem = st.enter_context(nc.semaphore('c_' + c.name))
            for e in ENG:
                k = 0
                for ins in self.q[e]:
                    if ins.signal and ins.chan is None:
                        k += 1
                        ins.signo = k
            block = st.enter_context(nc.Block())

            def replay(ename, eng):
                waited = {}
                q = self.q[ename]
                pending = None
                for ins in q:
                    for d in ins.deps:
                        if d.chan is not None:
                            c = d.chan
                            val = 16 * (c.count if c.group else d.cval)
                            key = ('c', id(c))
                            sem = c.sem
                        else:
                            val = d.signo
                            key = ('e', d.eng)
                            sem = esem[d.eng]
                        if waited.get(key, 0) >= val:
                            continue
                        waited[key] = val
                        eng.wait_ge(sem, val)
                    if ins.fn is None:
                        continue
                    bi = ins.fn(eng)
                    if ins.chan is not None:
                        bi.then_inc(ins.chan.sem, 16)
                    elif ins.signal:
                        bi.then_inc(esem[ename], 1)
                if ename == 'sp':
                    for d in final_waits:
                        c = d.chan
                        eng.wait_ge(c.sem, 16 * (c.count if c.group else d.cval))

            @block.tensor
            def _(e):
                replay('pe', e)

            @block.scalar
            def _(e):
                replay('act', e)

            @block.vector
            def _(e):
                replay('dve', e)

            @block.gpsimd
            def _(e):
                replay('pool', e)

            @block.sync
            def _(e):
                replay('sp', e)


class _Stop(Exception):
    pass


def build_program(stop=None, dumps=()):
    nc = bass.Bass("TRN2", target_bir_lowering=False)
    S = Sched(nc)
    REG = {}

    def din(name, shape):
        return nc.dram_tensor(name, list(shape), F32, kind="ExternalInput").ap()

    xl = din("xl", [8192, D])
    mem = din("mem", [256, D])
    w_in = din("w_in", [D, 4608])
    w_glu = din("w_glu", [512, 512])
    w_pa = din("w_pa", [512, D])
    w_pb = din("w_pb", [D, D])
    w_out = din("w_out", [D, D])
    w_q = din("w_q", [D, D])
    w_k = din("w_k", [D, D])
    w_v = din("w_v", [D, D])
    w_xo = din("w_xo", [D, D])
    w_gate = din("w_gate", [D, DFF])
    w_up = din("w_up", [D, DFF])
    w_down = din("w_down", [DFF, D])
    gvec = din("gvec", [5, D])
    lnv = din("lnv", [2, D])
    wsT_d = din("wsT", [128, 1024])
    sbias_d = din("sbias", [1, 1024])
    A_in = din("A_in", [5, 128, 512])
    B_in = din("B_in", [5, 128, 512])
    Cc_d = din("Cc", [128, 1024])
    dcol_d = din("dcol", [128, 4])
    bmask_d = din("bmask", [128, 128])
    pmask_d = din("pmask", [128, 2])
    out_d = nc.dram_tensor("out", [NTOK, D], F32, kind="ExternalOutput").ap()

    try:
      with ExitStack() as top:
        def sb(name, shape, dt, ctx=top):
            t = ctx.enter_context(nc.sbuf_tensor(name, list(shape), dt))
            REG[name] = (t, list(shape), dt)
            return t

        def checkpoint(name):
            if stop != name:
                return
            S.barrier()
            cd = S.chan("dump", group=True)
            lastd = []
            for dn in dumps:
                t, shape, dt = REG[dn]
                flat = int(np.prod(shape[1:]))
                dd = nc.dram_tensor("dbg_" + dn, [shape[0], flat], dt, kind="ExternalOutput").ap()
                src_ap = t[:]
                if len(shape) > 2:
                    letters = "abcdefg"[:len(shape) - 1]
                    src_ap = src_ap.rearrange("p " + " ".join(letters) + " -> p (" + " ".join(letters) + ")")
                lastd.append(S.op('sp', lambda e, dd=dd, src_ap=src_ap: e.dma_start(out=dd[:, :], in_=src_ap), chan=cd))
            S.emit(final_waits=lastd)
            raise _Stop()

        banks = [top.enter_context(nc.psum_tensor(f"bank{i}", [128, 512], F32)) for i in range(8)]
        bankR = [Res(f"bank{i}") for i in range(8)]
        bstate = {'i': 0}

        def nb():
            i = bstate['i']
            bstate['i'] = (i + 1) % 8
            return banks[i], bankR[i]

        ident = sb("ident", [128, 128], BF16)
        identf = sb("identf", [128, 128], F32)
        ones_b = sb("ones_b", [128, 128], BF16)
        kT = sb("kT", [128, 8, 256], BF16)
        vtok = sb("vtok", [128, 2, D], BF16)
        uT_own = sb("uT_own", [128, 4, NTOK], BF16)
        R_ident, R_gb, R_ln, R_ws, R_sbias = Res(), Res(), Res(), Res(), Res()
        R_kT, R_v = Res(), Res()
        R_uo = [Res(f"uo{g}") for g in range(4)]
        cur = {'name': "setup0"}

        def new_setup_chan(name):
            cur.clear()
            cur['name'] = name

        def cur_chan(eng):
            if eng not in cur:
                cur[eng] = S.chan(cur['name'] + "_" + eng, group=True)
            return cur[eng]

        def setup_load(eng, out_ap, in_ap, res, fresh=True):
            if fresh:
                ins = S.op(eng, lambda e: e.dma_start(out=out_ap, in_=in_ap), chan=cur_chan(eng))
                res.extra_w.append(ins)
            else:
                S.op(eng, lambda e: e.dma_start(out=out_ap, in_=in_ap), soft_w=[res], chan=cur_chan(eng))

        S.op('pool', lambda e: e.memset(ident[:], 0.0), w=[R_ident])
        S.op('pool', lambda e: e.affine_select(out=ident[:], in_=ident[:], pattern=[[-1, 128]],
                                               compare_op=ALU.not_equal, fill=1.0, base=0,
                                               channel_multiplier=1), r=[R_ident], w=[R_ident])
        S.op('pool', lambda e: e.memset(identf[:], 0.0), w=[R_ident])
        S.op('pool', lambda e: e.affine_select(out=identf[:], in_=identf[:], pattern=[[-1, 128]],
                                               compare_op=ALU.not_equal, fill=1.0, base=0,
                                               channel_multiplier=1), r=[R_ident], w=[R_ident])
        S.op('pool', lambda e: e.memset(ones_b[:], 1.0), w=[R_ident])

        def act(out, in_, func, r, w, scale=1.0, bias=0.0, accum=None):
            def f(e):
                kw = {}
                if accum is not None:
                    kw['accum_out'] = accum
                return e.activation(out=out, in_=in_, func=func, scale=scale, bias=bias, **kw)
            return S.op('act', f, r=r, w=w)

        def tt(eng, out, a, b, op, r, w):
            return S.op(eng, lambda e: e.tensor_tensor(out=out, in0=a, in1=b, op=op), r=r, w=w)

        def ts(eng, out, a, s1, s2, op0, op1, r, w):
            if s2 is None:
                return S.op(eng, lambda e: e.tensor_scalar(out=out, in0=a, scalar1=s1, scalar2=None, op0=op0), r=r, w=w)
            return S.op(eng, lambda e: e.tensor_scalar(out=out, in0=a, scalar1=s1, scalar2=s2, op0=op0, op1=op1), r=r, w=w)

        def stt(out, a, s, b, op0, op1, r, w):
            return S.op('dve', lambda e: e.scalar_tensor_tensor(out=out, in0=a, scalar=s, in1=b, op0=op0, op1=op1), r=r, w=w)

        def mm(out, lhsT, rhs, start, stop, r, w, tp=None):
            if tp is None:
                return S.op('pe', lambda e: e.matmul(out, lhsT=lhsT, rhs=rhs, start=start, stop=stop), r=r, w=w)
            return S.op('pe', lambda e: e.matmul(out, lhsT=lhsT, rhs=rhs, start=start, stop=stop, tile_position=tp), r=r, w=w)

        def tcopy(eng, out, in_, r, w):
            if eng == 'act':
                return S.op(eng, lambda e: e.activation(out=out, in_=in_, func=AF.Copy), r=r, w=w)
            return S.op(eng, lambda e: e.tensor_copy(out=out, in_=in_), r=r, w=w)

        def rmsnorm_to(out_bf, x_ap, g_ap, junk, ssum, rstd, r, w, Rtmp):
            act(junk, x_ap, AF.Square, r=r, w=[Rtmp], accum=ssum)
            act(rstd, ssum, AF.Sqrt, r=[Rtmp], w=[Rtmp], scale=1.0 / D, bias=epsb[:, 0:1])
            S.op('dve', lambda e: e.reciprocal(out=rstd, in_=rstd), r=[Rtmp], w=[Rtmp])
            return stt(out_bf, x_ap, rstd, g_ap, ALU.mult, ALU.mult, r=r + [Rtmp, R_gb], w=w)

        epsb = sb("epsb", [128, 1], F32)
        S.op('pool', lambda e: e.memset(epsb[:], EPS), w=[R_ident])

        def transpose_to(nT, col0, src_bf, r, w):
            bk, bR = nb()
            bkb = bk[:].bitcast(BF16)
            for k in range(8):
                S.op('pe', lambda e, k=k: e.transpose(out=bkb[:, k * 128:(k + 1) * 128],
                                                        in_=src_bf[:, k * 128:(k + 1) * 128],
                                                        identity=ident[:]),
                     r=r + [R_ident], w=[bR])
            tcopy('act', nT[:, :, col0:col0 + 128], bkb[:, :].rearrange("p (k c) -> p k c", k=8), r=[bR], w=w)

        NSLOT = 4
        with ExitStack() as ph:
            Wd = sb("Wd", [128, 2, 4, 2, 8, 128], BF16, ph)
            Wo = sb("Wo", [128, 2, 2, 16, 8, 32], BF16, ph)
            Kt = sb("Kt", [128, 4, 15, 128], BF16, ph)
            QT = sb("QT", [128, 3, 9, 3, 32], F32, ph)
            q4096 = sb("q4096", [128, 3, 32], F32, ph)
            R_Wd, R_Wo, R_Kt, R_QT = Res(), Res(), Res(), Res()
            S.op('pool', lambda e: e.memset(QT[:], 0.0), w=[R_QT])
            R_ut = [Res(f"ut{g}") for g in range(4)]

            with ExitStack() as ps_:
                arr_cache = {}

                def arr(name, n=512):
                    if name not in arr_cache:
                        arr_cache[name] = sb("S_" + name, [128, n], F32, ps_)
                    return arr_cache[name]
                Rs = Res("S")

                def lam_derive(src, pre, levels=False, fresh=True, extra_loads=()):
                    lre, lim, ls = arr("lre"), arr("lim"), arr("ls")
                    for i, t in enumerate([lre, lim, ls]):
                        setup_load('sp', t[:], src[i, :, :], Rs, fresh)
                    for (t, i) in extra_loads:
                        setup_load('sp', t[:], src[i, :, :], Rs, fresh)
                    R1 = ([Rs], [Rs])
                    x, h, dt_ = arr("x"), arr("h"), arr("dt")
                    ts('dve', x[:], ls[:], -1.0 / 128, None, ALU.mult, None, *R1)
                    ts('dve', h[:], x[:], 1.0 / 5, 1.0, ALU.mult, ALU.add, *R1)
                    for k in (4, 3, 2):
                        tt('dve', h[:], h[:], x[:], ALU.mult, *R1)
                        ts('dve', h[:], h[:], 1.0 / k, 1.0, ALU.mult, ALU.add, *R1)
                    tt('dve', h[:], h[:], x[:], ALU.mult, *R1)
                    for _ in range(7):
                        stt(h[:], h[:], 2.0, h[:], ALU.add, ALU.mult, *R1)
                    ts('dve', h[:], h[:], 1.0, None, ALU.add, None, *R1)
                    S.op('dve', lambda e: e.reciprocal(out=dt_[:], in_=h[:]), r=[Rs], w=[Rs])
                    a, b = arr("za"), arr("zb")
                    tt('dve', a[:], lre[:], dt_[:], ALU.mult, *R1)
                    ts('dve', a[:], a[:], 1.0 / 64, None, ALU.mult, None, *R1)
                    tt('dve', b[:], lim[:], dt_[:], ALU.mult, *R1)
                    ts('dve', b[:], b[:], 1.0 / 64, None, ALU.mult, None, *R1)
                    hr, hi, u1, u2, u3 = arr("hr"), arr("hi"), arr("u1"), arr("u2"), arr("u3")
                    ts('dve', hr[:], a[:], 1.0 / 7, 1.0, ALU.mult, ALU.add, *R1)
                    ts('dve', hi[:], b[:], 1.0 / 7, None, ALU.mult, None, *R1)

                    def zmul_h():
                        tt('dve', u1[:], a[:], hr[:], ALU.mult, *R1)
                        tt('dve', u2[:], b[:], hi[:], ALU.mult, *R1)
                        tt('dve', u1[:], u1[:], u2[:], ALU.subtract, *R1)
                        tt('dve', u3[:], a[:], hi[:], ALU.mult, *R1)
                        tt('dve', u2[:], b[:], hr[:], ALU.mult, *R1)
                        tt('dve', u3[:], u3[:], u2[:], ALU.add, *R1)
                    for k in (6, 5, 4, 3, 2):
                        zmul_h()
                        ts('dve', hr[:], u1[:], 1.0 / k, 1.0, ALU.mult, ALU.add, *R1)
                        ts('dve', hi[:], u3[:], 1.0 / k, None, ALU.mult, None, *R1)
                    zmul_h()
                    wr, wi = u1, u3

                    def square():
                        ts('dve', u2[:], wr[:], 2.0, None, ALU.add, None, *R1)
                        tt('dve', hr[:], wr[:], u2[:], ALU.mult, *R1)
                        tt('dve', hi[:], wi[:], wi[:], ALU.mult, *R1)
                        tt('dve', u2[:], u2[:], wr[:], ALU.add, *R1)
                        tt('dve', wr[:], hr[:], hi[:], ALU.subtract, *R1)
                        tt('dve', wi[:], wi[:], u2[:], ALU.mult, *R1)
                    for _ in range(6):
                        square()
                    lbr, lbi, lm1 = arr("lbr"), arr("lbi"), arr("lm1")
                    ts('dve', lbr[:], wr[:], 1.0, None, ALU.add, None, *R1)
                    tcopy('dve', lbi[:], wi[:], *R1)
                    tcopy('dve', lm1[:], wr[:], *R1)
                    if levels:
                        wrc = wr[:].rearrange("p (g c) -> p g c", c=16)[:, :, 0]
                        wic = wi[:].rearrange("p (g c) -> p g c", c=16)[:, :, 0]
                        for lvl in range(4):
                            for _ in range(3):
                                square()
                            if lvl < 3:
                                ts('dve', QT[:, lvl, 1, 0, :], wrc, 1.0, None, ALU.add, None, [Rs], [Rs, R_QT])
                                tcopy('dve', QT[:, lvl, 1, 1, :], wic, [Rs], [Rs, R_QT])
                            else:
                                ts('dve', q4096[:, 0, :], wrc, 1.0, None, ALU.add, None, [Rs], [Rs, R_QT])
                                tcopy('dve', q4096[:, 1, :], wic, [Rs], [Rs, R_QT])
                    return lre, lim, lbr, lbi, lm1

                def cmul(o_re, o_im, a_re, a_im, b_re, b_im, t1, t2):
                    tt('dve', t1, a_im, b_im, ALU.mult, [Rs], [Rs])
                    tt('dve', t2, a_im, b_re, ALU.mult, [Rs], [Rs])
                    tt('dve', o_re, a_re, b_re, ALU.mult, [Rs], [Rs])
                    tt('dve', o_re, o_re, t1, ALU.subtract, [Rs], [Rs])
                    tt('dve', o_im, a_re, b_im, ALU.mult, [Rs], [Rs])
                    tt('dve', o_im, o_im, t2, ALU.add, [Rs], [Rs])

                checkpoint('S0')
                lre, lim, lbr, lbi, lm1 = lam_derive(A_in, "A")
                checkpoint('S1')
                bre, bim = arr("Abre"), arr("Abim")
                setup_load('sp', bre[:], A_in[3, :, :], Rs)
                setup_load('sp', bim[:], A_in[4, :, :], Rs)
                bmask = sb("bmask_s", [128, 128], F32, ps_)
                pmask = sb("pmask_s", [128, 2], F32, ps_)
                dcol = sb("dcol_s", [128, 4], F32, ps_)
                Cc = sb("Cc_s", [128, 1024], F32, ps_)
                setup_load('sp', bmask[:], bmask_d[:, :], Rs)
                setup_load('sp', pmask[:], pmask_d[:, :], Rs)
                setup_load('sp', dcol[:], dcol_d[:, :], Rs)
                setup_load('sp', Cc[:], Cc_d[:, :], Rs)
                act(Cc[64:128, :], Cc[64:128, :], AF.Copy, r=[Rs], w=[Rs], scale=-1.0)
                t1, t2, t3, t4 = arr("t1"), arr("t2"), arr("t3"), arr("t4")
                den = arr("den")
                tt('dve', den[:], lre[:], lre[:], ALU.mult, [Rs], [Rs])
                tt('dve', t1[:], lim[:], lim[:], ALU.mult, [Rs], [Rs])
                tt('dve', den[:], den[:], t1[:], ALU.add, [Rs], [Rs])
                S.op('dve', lambda e: e.reciprocal(out=den[:], in_=den[:]), r=[Rs], w=[Rs])
                cre_, cim_ = arr("coefre"), arr("coefim")
                tt('dve', t1[:], lm1[:], lre[:], ALU.mult, [Rs], [Rs])
                tt('dve', t2[:], lbi[:], lim[:], ALU.mult, [Rs], [Rs])
                tt('dve', t1[:], t1[:], t2[:], ALU.add, [Rs], [Rs])
                tt('dve', cre_[:], t1[:], den[:], ALU.mult, [Rs], [Rs])
                tt('dve', t1[:], lbi[:], lre[:], ALU.mult, [Rs], [Rs])
                tt('dve', t2[:], lm1[:], lim[:], ALU.mult, [Rs], [Rs])
                tt('dve', t1[:], t1[:], t2[:], ALU.subtract, [Rs], [Rs])
                tt('dve', cim_[:], t1[:], den[:], ALU.mult, [Rs], [Rs])
                wre = [arr("wre0"), arr("wre1")]
                wim = [arr("wim0"), arr("wim1")]
                cmul(wre[0][:], wim[0][:], cre_[:], cim_[:], bre[:], bim[:], t1[:], t2[:])
                tstk = sb("tstk", [128, 128], F32, ps_)
                tT = sb("tT", [128, 128], F32, ps_)
                kacc = sb("kacc", [128, 4, 128], F32, ps_)
                for n in range(8):
                    cur_re, cur_im = wre[n % 2], wim[n % 2]
                    v_re = cur_re[:].rearrange("p (d g q) -> p d g q", d=2, g=4)
                    v_im = cur_im[:].rearrange("p (d g q) -> p d g q", d=2, g=4)
                    for d_ in range(2):
                        j = 7 - n if d_ == 0 else n
                        for ri, v in ((0, v_re), (1, v_im)):
                            for g2 in range(2):
                                ts('dve', Wd[:, d_, :, ri, j, g2 * 64:(g2 + 1) * 64], v[:, d_, :, :],
                                   pmask[:, g2:g2 + 1], None, ALU.mult, None, [Rs], [Rs, R_Wd])
                        for gt in range(4):
                            tcopy('dve', tstk[:, 0:64], v_re[:, d_, gt, :], [Rs], [Rs])
                            tcopy('dve', tstk[:, 64:128], v_im[:, d_, gt, :], [Rs], [Rs])
                            bk, bR = nb()
                            S.op('pe', lambda e, bk=bk: e.transpose(out=bk[:, 0:128], in_=tstk[:], identity=identf[:]),
                                 r=[Rs, R_ident], w=[bR])
                            tcopy('act', tT[:], bk[:, 0:128], [bR], [Rs])
                            bk2, bR2 = nb()
                            cc = Cc[:, (d_ * 4 + gt) * 128:(d_ * 4 + gt + 1) * 128]
                            mm(bk2[:, 0:128], tT[:], cc, True, True, [Rs], [bR2])
                            if n == 0:
                                if d_ == 0:
                                    tt('dve', kacc[:, gt, :], bk2[:, 0:128], bmask[:], ALU.mult, [bR2, Rs], [Rs])
                                    stt(kacc[:, gt, :], identf[:], dcol[:, gt:gt + 1], kacc[:, gt, :], ALU.mult, ALU.add,
                                        [Rs, R_ident], [Rs])
                                else:
                                    tt('dve', tstk[:], bk2[:, 0:128], bmask[:], ALU.mult, [bR2, Rs], [Rs])
                                    tt('dve', Kt[:, gt, 0, :], tstk[:], kacc[:, gt, :], ALU.add, [Rs], [Rs, R_Kt])
                            else:
                                tt('dve', Kt[:, gt, d_ * 7 + n, :], bk2[:, 0:128], bmask[:], ALU.mult, [bR2, Rs], [Rs, R_Kt])
                    if n < 7:
                        nxt_re, nxt_im = wre[(n + 1) % 2], wim[(n + 1) % 2]
                        cmul(nxt_re[:], nxt_im[:], lbr[:], lbi[:], cur_re[:], cur_im[:], t1[:], t2[:])

                checkpoint('S2')
                new_setup_chan("setupB")
                creB, cimB = arr("Abre"), arr("Abim")
                lreB, limB, lbrB, lbiB, _ = lam_derive(B_in, "B", levels=True, fresh=False,
                                                       extra_loads=[(creB, 3), (cimB, 4)])
                S.op('pool', lambda e: e.memset(Wo[:], 0.0), w=[R_Wo])
                pw_re = [arr("wre0"), arr("wre1")]
                pw_im = [arr("wim0"), arr("wim1")]
                tcopy('dve', pw_re[1][:], lbrB[:], [Rs], [Rs])
                tcopy('dve', pw_im[1][:], lbiB[:], [Rs], [Rs])
                for n in range(1, 9):
                    pr, pi_ = pw_re[n % 2], pw_im[n % 2]
                    cmul(t3[:], t4[:], creB[:], cimB[:], pr[:], pi_[:], t1[:], t2[:])
                    ts('dve', t4[:], t4[:], -1.0, None, ALU.mult, None, [Rs], [Rs])
                    for d_ in range(2):
                        tl = n - 1 if d_ == 0 else 8 - n
                        for ri, src in ((0, t3), (1, t4)):
                            sv = src[:].rearrange("p (d g c) -> p d g c", d=2, g=16)
                            for g2 in range(2):
                                tcopy('dve', Wo[g2 * 64:(g2 + 1) * 64, ri, d_, :, tl, g2 * 16:(g2 + 1) * 16],
                                      sv[g2 * 64:(g2 + 1) * 64, d_, :, :], [Rs], [Rs, R_Wo])
                    if n < 8:
                        cmul(pw_re[(n + 1) % 2][:], pw_im[(n + 1) % 2][:], lbrB[:], lbiB[:], pr[:], pi_[:], t1[:], t2[:])
                checkpoint('S3')
                for lvl in range(3):
                    for m in range(2, 9):
                        cmul(QT[:, lvl, m, 0, :], QT[:, lvl, m, 1, :], QT[:, lvl, 1, 0, :], QT[:, lvl, 1, 1, :],
                             QT[:, lvl, m - 1, 0, :], QT[:, lvl, m - 1, 1, :], t1[:, 0:32], t2[:, 0:32])
                for lvl in range(3):
                    ts('dve', QT[:, lvl, 1:9, 2, :], QT[:, lvl, 1:9, 1, :], -1.0, None, ALU.mult, None, [Rs], [Rs, R_QT])
                ts('dve', q4096[:, 2, :], q4096[:, 1, :], -1.0, None, ALU.mult, None, [Rs], [Rs, R_QT])

                checkpoint('S')
                S.barrier()
            with ExitStack() as pm_:
                new_setup_chan("setupM")
                Rs = Res("M")
                gmem = sb("gmem", [128, D], F32, pm_)
                setup_load('sp', gmem[:], gvec[2:3, :].partition_broadcast(128), R_gb)
                memt = sb("memt", [128, 2, D], F32, pm_)
                memb = sb("memb", [128, 2, D], BF16, pm_)
                memT = sb("memT", [128, 8, 256], BF16, pm_)
                wkv = sb("wkv", [128, 8, 512], BF16, pm_)
                junk = sb("junkS", [128, D], BF16, pm_)
                ssum = sb("ssumS", [128, 1], F32, pm_)
                rstd = sb("rstdS", [128, 1], F32, pm_)
                setup_load('sp', memt[:], mem.rearrange("(t p) d -> p t d", p=128), Rs)
                for t_ in range(2):
                    rmsnorm_to(memb[:, t_, :], memt[:, t_, :], gmem[:], junk[:], ssum[:], rstd[:], [Rs], [Rs], Rs)
                    transpose_to(memT, t_ * 128, memb[:, t_, :], [Rs], [Rs])
                c_kv = S.chan("kv")
                for half in range(2):
                    S.op('pool', lambda e, half=half: e.dma_start(
                        out=wkv[:], in_=w_k[:, half * 512:(half + 1) * 512].rearrange("(k p) c -> p k c", p=128)),
                        w=[Rs], chan=c_kv)
                    for m_ in range(4):
                        bk, bR = nb()
                        for k in range(8):
                            mm(bk[:, 0:256], wkv[:, k, m_ * 128:(m_ + 1) * 128], memT[:, k, :], k == 0, k == 7, [Rs], [bR])
                        tcopy('act', kT[:, half * 4 + m_, :], bk[:, 0:256], [bR], [R_kT])
                for half in range(2):
                    S.op('pool', lambda e, half=half: e.dma_start(
                        out=wkv[:], in_=w_v[:, half * 512:(half + 1) * 512].rearrange("(k p) c -> p k c", p=128)),
                        w=[Rs], chan=c_kv)
                    for t_ in range(2):
                        bk, bR = nb()
                        for k in range(8):
                            mm(bk[:, :], memT[:, k, t_ * 128:(t_ + 1) * 128], wkv[:, k, :], k == 0, k == 7, [Rs], [bR])
                        tcopy('act', vtok[:, t_, half * 512:(half + 1) * 512], bk[:, :], [bR], [R_v])
                checkpoint('M')
                S.barrier()

            uT_oth = sb("uT_oth", [128, 4, NTOK], BF16, ph)
            with ExitStack() as pa_:
                new_setup_chan("setupA")
                gmix = sb("gmix", [128, D], F32, pa_)
                setup_load('sp', gmix[:], gvec[0:1, :].partition_broadcast(128), R_gb)
                xs = [sb(f"xsA{i}", [128, D], F32, pa_) for i in range(2)]
                R_xs = [Res() for _ in range(2)]
                c_xs = [S.chan(f"xsA{i}") for i in range(2)]
                nb_ = [sb(f"nbA{i}", [128, D], BF16, pa_) for i in range(2)]
                R_nb = [Res() for _ in range(2)]
                nT = [sb(f"nTA{i}", [128, 8, 512], BF16, pa_) for i in range(2)]
                R_nT = [Res() for _ in range(2)]
                wxa = sb("wxa", [128, 8, 512], BF16, pa_)
                R_wxa = Res()
                junk = sb("junkA", [128, D], BF16, pa_)
                ssum = sb("ssumA", [128, 1], F32, pa_)
                rstd = sb("rstdA", [128, 1], F32, pa_)
                R_tmp = Res()
                c_wxa = S.chan("wxa")
                S.op('pool', lambda e: e.dma_start(out=wxa[:], in_=w_in[:, 0:512].rearrange("(k p) c -> p k c", p=128)),
                     w=[R_wxa], chan=c_wxa)
                for blk in range(16):
                    nTb, RnT = nT[blk % 2], R_nT[blk % 2]
                    for t_ in range(4):
                        i = (blk * 4 + t_)
                        tok0 = blk * 512 + t_ * 128
                        x_, Rx, cx = xs[i % 2], R_xs[i % 2], c_xs[i % 2]
                        S.op('sp', lambda e, x_=x_, tok0=tok0: e.dma_start(out=x_[:], in_=xl[tok0:tok0 + 128, :]),
                             w=[Rx], chan=cx)
                        nbb, Rnb = nb_[i % 2], R_nb[i % 2]
                        rmsnorm_to(nbb[:], x_[:], gmix[:], junk[:], ssum[:], rstd[:], [Rx], [Rnb], R_tmp)
                        transpose_to(nTb, t_ * 128, nbb[:], [Rnb], [RnT])
                    for m_ in range(4):
                        bk, bR = nb()
                        for k in range(8):
                            mm(bk[:, :], wxa[:, k, m_ * 128:(m_ + 1) * 128], nTb[:, k, :], k == 0, k == 7, [R_wxa, RnT], [bR])
                        if blk < 8:
                            tcopy('act' if m_ % 2 == 0 else 'dve', uT_own[:, m_, blk * 512:(blk + 1) * 512], bk[:, :], [bR], [R_uo[m_]])
                        else:
                            tcopy('act' if m_ % 2 == 0 else 'dve', uT_oth[:, m_, (blk - 8) * 512:(blk - 7) * 512], bk[:, :], [bR], [R_ut[m_]])
                checkpoint('A')
                S.barrier()

            with ExitStack() as pb_:
                Dt = [sb(f"Dt{i}", [128, 2, 1024], F32, pb_) for i in range(2)]
                R_D = [Res() for _ in range(2)]
                Sb = sb("Sb", [128, 8, 2, 512], BF16, pb_)
                R_Sb = [Res() for _ in range(8)]
                ytmp = sb("ytmp", [128, 4, 512], BF16, pb_)
                R_yt = Res()
                ykeep = sb("ykeep", [128, 4, 512], BF16, pb_)
                R_yk = Res()
                tile_i = 0
                for gt in range(4):
                    uo = uT_own[:, gt, :].rearrange("p (k t) -> p k t", t=8)
                    ut = uT_oth[:, gt, :].rearrange("p (k t) -> p k t", t=8)
                    for d_ in range(2):
                        for gpl in range(4):
                            gp = gt * 4 + gpl
                            Dtile, RD = Dt[tile_i % 2], R_D[tile_i % 2]
                            tile_i += 1
                            nk = 512 if d_ == 0 else 1024
                            halves = [(uo, R_uo[gt], 0)] + ([(ut, R_ut[gt], 512)] if d_ == 1 else [])
                            for (uv, Ru, c0) in halves:
                                for ri in range(2):
                                    bk, bR = nb()
                                    for j in range(8):
                                        mm(bk[:, :], Wd[32 * gpl:32 * gpl + 32, d_, gt, ri, j, :], uv[32 * gpl:32 * gpl + 32, :, j],
                                           j == 0, j == 7, [R_Wd, Ru], [bR], tp=(32 * gpl, 0))
                                    tcopy('act', Dtile[:, ri, c0:c0 + 512], bk[:, :], [bR], [RD])
                            qi = d_ * 16 + gp

                            def scal(lvl, m, which, qi=qi):
                                if lvl == 3:
                                    return q4096[:, which, qi:qi + 1]
                                return QT[:, lvl, m, which, qi:qi + 1]

                            def cmac(dst_c, src_c, lvl, m, cnt, st_dst, st_src, Dtile=Dtile, RD=RD):
                                def view(ri, c):
                                    start, count, stride = c
                                    if count == 1 or stride == 1:
                                        return Dtile[:, ri, start:start + count]
                                    rr_ = max(0, start + count * stride - 1024)
                                    b0 = start - rr_
                                    return Dtile[:, ri, b0:b0 + count * stride].rearrange("p (c s) -> p c s", s=stride)[:, :, rr_]
                                dre, dim_ = view(0, dst_c), view(1, dst_c)
                                sre, sim = view(0, src_c), view(1, src_c)
                                rr = [RD, R_QT]
                                stt(dre, sre, scal(lvl, m, 0), dre, ALU.mult, ALU.add, rr, [RD])
                                stt(dre, sim, scal(lvl, m, 2), dre, ALU.mult, ALU.add, rr, [RD])
                                stt(dim_, sim, scal(lvl, m, 0), dim_, ALU.mult, ALU.add, rr, [RD])
                                stt(dim_, sre, scal(lvl, m, 1), dim_, ALU.mult, ALU.add, rr, [RD])

                            def cscan(off, st, n, lvl, rev):
                                if n <= 8:
                                    order = range(1, n) if not rev else range(n - 2, -1, -1)
                                    for jx in order:
                                        prev = jx - 1 if not rev else jx + 1
                                        cmac((off + jx * st, 1, 1), (off + prev * st, 1, 1), lvl, 1, 1, 0, 0)
                                    return
                                M = n // 8
                                order = range(1, 8) if not rev else range(6, -1, -1)
                                for i_ in order:
                                    prev = i_ - 1 if not rev else i_ + 1
                                    cmac((off + i_ * st, M, 8 * st), (off + prev * st, M, 8 * st), lvl, 1, M, 0, 0)
                                endpos = 7 if not rev else 0
                                cscan(off + endpos * st, 8 * st, M, lvl + 1, rev)
                                if not rev:
                                    for i_ in range(7):
                                        cmac((off + (8 + i_) * st, M - 1, 8 * st), (off + 7 * st, M - 1, 8 * st), lvl, i_ + 1, M - 1, 0, 0)
                                else:
                                    for i_ in range(1, 8):
                                        cmac((off + i_ * st, M - 1, 8 * st), (off + 8 * st, M - 1, 8 * st), lvl, 8 - i_, M - 1, 0, 0)

                            cscan(0, 1, nk, 0, d_ == 1)
                            sidx = d_ * 4 + gpl
                            for ri in range(2):
                                if d_ == 0:
                                    S.op('pool', lambda e, sidx=sidx, ri=ri: e.memset(Sb[:, sidx, ri, 0:1], 0.0), w=[R_Sb[sidx]])
                                    tcopy('act', Sb[:, sidx, ri, 1:512], Dtile[:, ri, 0:511], [RD], [R_Sb[sidx]])
                                else:
                                    tcopy('act', Sb[:, sidx, ri, :], Dtile[:, ri, 1:513], [RD], [R_Sb[sidx]])
                    for hp in range(2):
                        bks = [nb() for _ in range(4)]
                        for ti in range(4):
                            tl = hp * 4 + ti
                            bk, bR = bks[ti]
                            first = True
                            mm(bk[:, :], Kt[:, gt, 0, :], uo[:, :, tl], True, False, [R_Kt, R_uo[gt]], [bR])
                            for tau in range(1, tl + 1):
                                mm(bk[:, :], Kt[:, gt, tau, :], uo[:, :, tl - tau], False, False, [R_Kt, R_uo[gt]], [bR])
                            for tau in range(1, 8 - tl):
                                mm(bk[:, :], Kt[:, gt, 7 + tau, :], uo[:, :, tl + tau], False, False, [R_Kt, R_uo[gt]], [bR])
                            for gpl in range(4):
                                for d_ in range(2):
                                    gp = gt * 4 + gpl
                                    sidx = d_ * 4 + gpl
                                    for ri in range(2):
                                        mm(bk[32 * gpl:32 * gpl + 32, :], Wo[:, ri, d_, gp, tl, :], Sb[:, sidx, ri, :],
                                           False, d_ == 1 and ri == 1, [R_Wo, R_Sb[sidx]], [bR], tp=(0, 32 * gpl))
                            act(ytmp[:, ti, :], bk[:, :], AF.Gelu_apprx_tanh, r=[bR], w=[R_yt])
                        if hp == 1:
                            pass
                        if hp == 0:
                            tcopy('pool', ykeep[:], ytmp[:], [R_yt], [R_yk])
                        else:
                            for ti in range(4):
                                tcopy('pool', uo[:, :, ti], ykeep[:, ti, :], [R_yk], [R_uo[gt]])
                                tcopy('pool', uo[:, :, 4 + ti], ytmp[:, ti, :], [R_yt], [R_uo[gt]])
                checkpoint('B')
                S.barrier()

        with ExitStack() as pc_:
            hb = [sb(f"hb{i}", [128, D], F32, pc_) for i in range(4)]
            R_h = [Res() for _ in range(4)]
            c_h = [S.chan(f"h{i}") for i in range(4)]
            c_o = [S.chan(f"o{i}") for i in range(4)]
            nbt = [sb(f"nbC{i}", [128, D], BF16, pc_) for i in range(2)]
            R_nbt = [Res() for _ in range(2)]
            nT = sb("nTC", [128, 8, 512], BF16, pc_)
            R_nT = [Res() for _ in range(4)]
            wsl = [sb(f"wsl{i}", [128, 8, 512], BF16, pc_) for i in range(NSLOT)]
            R_ws_ = [Res() for _ in range(NSLOT)]
            c_ws = [S.chan(f"ws{i}") for i in range(NSLOT)]
            wst = {'i': 0}
            scr = sb("scr", [128, 12, 4, 512], BF16, pc_)
            R_scr = [Res() for _ in range(12)]
            zvf = [sb(f"zvf{i}", [128, D], F32, pc_) for i in range(2)]
            R_zvf = [Res() for _ in range(2)]
            zvn = [sb(f"zvn{i}", [128, D], BF16, pc_) for i in range(2)]
            R_zvn = [Res() for _ in range(2)]
            junk = sb("junkC", [128, D], BF16, pc_)
            new_setup_chan("setupC")
            g4 = sb("g4", [128, 4, D], F32, pc_)
            for i_, gi_ in enumerate((0, 1, 3, 4)):
                setup_load('sp', g4[:, i_, :], gvec[gi_:gi_ + 1, :].partition_broadcast(128), R_gb)
            ln_t = sb("ln_t", [128, 2, D], F32, pc_)
            for i_ in range(2):
                setup_load('sp', ln_t[:, i_, :], lnv[i_:i_ + 1, :].partition_broadcast(128), R_ln)
            wsT = sb("wsT_s", [128, 1024], BF16, pc_)
            sbias = sb("sbias_s", [1, 1024], BF16, pc_)
            setup_load('pool', wsT[:], wsT_d[:, :], R_ws)
            setup_load('pool', sbias[:], sbias_d[:, :], R_sbias)
            ssum = sb("ssumC", [128, 4], F32, pc_)
            rstd = sb("rstdC", [128, 4], F32, pc_)
            R_tmp = Res()
            ftmp = [sb(f"ftmp{i}", [128, 512], F32, pc_) for i in range(4)]
            R_ft = [Res() for _ in range(4)]
            fst = {'i': 0}

            def wload(w_ap, r0, nk, c0, ncol):
                i = wst['i']
                wst['i'] = (i + 1) % NSLOT
                sl, Rsl, ch = wsl[i], R_ws_[i], c_ws[i]
                S.op('pool', lambda e: e.dma_start(
                    out=sl[:, 0:nk, 0:ncol],
                    in_=w_ap[r0 * 128:(r0 + nk) * 128, c0:c0 + ncol].rearrange("(k p) c -> p k c", p=128)),
                    w=[Rsl], chan=ch)
                return sl, Rsl

            def nft():
                i = fst['i']
                fst['i'] = (i + 1) % 4
                return ftmp[i], R_ft[i]

            def norm_block(gi):
                for t_ in range(4):
                    nbb, Rnb = nbt[t_ % 2], R_nbt[t_ % 2]
                    rmsnorm_to(nbb[:], hb[t_][:], g4[:, gi, :], junk[:], ssum[:, 0:1], rstd[:, 0:1], [R_h[t_]], [Rnb], R_tmp)
                    transpose_to(nT, t_ * 128, nbb[:], [Rnb], [R_nT[t_]])

            def proj_fm(w_ap, c0, nmt, evac):
                sl, Rsl = wload(w_ap, 0, 8, c0, nmt * 128)
                for m_ in range(nmt):
                    bk, bR = nb()
                    for k in range(8):
                        mm(bk[:, :], sl[:, k, m_ * 128:(m_ + 1) * 128], nT[:, k, :], k == 0, k == 7, [Rsl] + R_nT, [bR])
                    evac(m_, bk, bR)

            def out_proj_tm(w_ap, nkt, srcT, srcR):
                for half in range(2):
                    chunks = []
                    r0 = 0
                    while r0 < nkt:
                        nk = min(8, nkt - r0)
                        chunks.append((r0, nk))
                        r0 += nk
                    bks = [nb() for _ in range(4)]
                    for (r0, nk) in chunks:
                        sl, Rsl = wload(w_ap, r0, nk, half * 512, 512)
                        for t_ in range(4):
                            bk, bR = bks[t_]
                            for k in range(nk):
                                kk = r0 + k
                                mm(bk[:, :], srcT(kk)[:, t_ * 128:(t_ + 1) * 128], sl[:, k, :], kk == 0, kk == nkt - 1,
                                   [Rsl] + srcR(kk), [bR])
                    for t_ in range(4):
                        bk, bR = bks[t_]
                        hv = hb[t_][:, half * 512:(half + 1) * 512]
                        tt('dve', hv, hv, bk[:, :], ALU.add, [bR, R_h[t_]], [R_h[t_]])

            sc = lambda s_: scr[:, s_, :, :]
            for blk in range(8):
                tok0 = blk * 512
                for t_ in range(4):
                    S.op('sp', lambda e, t_=t_, tok0=tok0: e.dma_start(out=hb[t_][:], in_=xl[tok0 + t_ * 128:tok0 + (t_ + 1) * 128, :]),
                         w=[R_h[t_]], chan=c_h[t_])
                norm_block(0)
                for part, (c0, func) in enumerate(((512, AF.Gelu_apprx_tanh), (2560, AF.Sigmoid), (3584, AF.Sigmoid))):
                    for half in range(2):
                        s_ = part * 2 + half

                        def ev(m_, bk, bR, s_=s_, func=func):
                            act(scr[:, s_, m_, :], bk[:, :], func, r=[bR], w=[R_scr[s_]])
                        proj_fm(w_in, c0 + half * 512, 4, ev)
                zsl = [wload(w_in, 0, 8, 1536 + half * 512, 512) for half in range(2)]
                for t_ in range(4):
                    zf, Rzf = zvf[t_ % 2], R_zvf[t_ % 2]
                    zn, Rzn = zvn[t_ % 2], R_zvn[t_ % 2]
                    for half in range(2):
                        sl, Rsl = zsl[half]
                        bk, bR = nb()
                        for k in range(8):
                            mm(bk[:, :], nT[:, k, t_ * 128:(t_ + 1) * 128], sl[:, k, :], k == 0, k == 7, [Rsl] + R_nT, [bR])
                        act(zf[:, half * 512:(half + 1) * 512], bk[:, :], AF.Gelu_apprx_tanh, r=[bR], w=[Rzf, R_tmp],
                            accum=ssum[:, half:half + 1])
                    act(junk[:], zf[:], AF.Square, r=[Rzf], w=[R_tmp], accum=ssum[:, 2:3])
                    tt('dve', ssum[:, 0:1], ssum[:, 0:1], ssum[:, 1:2], ALU.add, [R_tmp], [R_tmp])
                    ts('dve', rstd[:, 1:2], ssum[:, 0:1], 1.0 / D, None, ALU.mult, None, [R_tmp], [R_tmp])
                    tt('dve', rstd[:, 2:3], rstd[:, 1:2], rstd[:, 1:2], ALU.mult, [R_tmp], [R_tmp])
                    stt(rstd[:, 2:3], ssum[:, 2:3], 1.0 / D, rstd[:, 2:3], ALU.mult, ALU.subtract, [R_tmp], [R_tmp])
                    act(rstd[:, 2:3], rstd[:, 2:3], AF.Sqrt, r=[R_tmp], w=[R_tmp], bias=epsb[:, 0:1])
                    S.op('dve', lambda e: e.reciprocal(out=rstd[:, 2:3], in_=rstd[:, 2:3]), r=[R_tmp], w=[R_tmp])
                    ts('dve', zf[:], zf[:], rstd[:, 1:2], rstd[:, 2:3], ALU.subtract, ALU.mult, [R_tmp, Rzf], [Rzf])
                    tt('dve', zf[:], zf[:], ln_t[:, 0, :], ALU.mult, [Rzf, R_ln], [Rzf])
                    tt('dve', zn[:], zf[:], ln_t[:, 1, :], ALU.add, [Rzf, R_ln], [Rzn])
                    for h_ in range(8):
                        bk, bR = nb()
                        mm(bk[:, 0:128], zn[:, h_ * 128:(h_ + 1) * 128], wsT[:, h_ * 128:(h_ + 1) * 128], True, False, [Rzn, R_ws], [bR])
                        mm(bk[:, 0:128], ones_b[0:1, :], sbias[0:1, h_ * 128:(h_ + 1) * 128], False, True, [R_ident, R_sbias], [bR])
                        s_ = 6 + h_ // 4
                        zu_ = scr[:, h_ // 4, h_ % 4, t_ * 128:(t_ + 1) * 128]
                        tt('dve', scr[:, s_, h_ % 4, t_ * 128:(t_ + 1) * 128], bk[:, 0:128], zu_, ALU.mult,
                           [bR, R_scr[h_ // 4]], [R_scr[s_]])
                sl, Rsl = wload(w_glu, 0, 4, 0, 512)
                for m_ in range(4):
                    bk, bR = nb()
                    for k in range(4):
                        mm(bk[:, :], sl[:, k, m_ * 128:(m_ + 1) * 128], uT_own[:, k, tok0:tok0 + 512], k == 0, k == 3,
                           [Rsl, R_uo[k]], [bR])
                    ft, Rf = nft()
                    act(ft[:], bk[:, :], AF.Sigmoid, r=[bR], w=[Rf])
                    tt('dve', scr[:, 8, m_, :], ft[:], uT_own[:, m_, tok0:tok0 + 512], ALU.mult, [Rf, R_uo[m_]], [R_scr[8]])
                for half in range(2):
                    sla, Rsla = wload(w_pa, 0, 4, half * 512, 512)
                    slb, Rslb = wload(w_pb, 0, 8, half * 512, 512)
                    for m_ in range(4):
                        bka, bRa = nb()
                        for k in range(4):
                            mm(bka[:, :], sla[:, k, m_ * 128:(m_ + 1) * 128], scr[:, 8, k, :], k == 0, k == 3, [Rsla, R_scr[8]], [bRa])
                        bkb, bRb = nb()
                        for k in range(8):
                            mm(bkb[:, :], slb[:, k, m_ * 128:(m_ + 1) * 128], scr[:, 6 + k // 4, k % 4, :], k == 0, k == 7,
                               [Rslb, R_scr[6 + k // 4]], [bRb])
                        ft, Rf = nft()
                        tt('dve', ft[:], bka[:, :], scr[:, 2 + half, m_, :], ALU.mult, [bRa, R_scr[2 + half]], [Rf])
                        ft2, Rf2 = nft()
                        tt('dve', ft2[:], bkb[:, :], scr[:, 4 + half, m_, :], ALU.mult, [bRb, R_scr[4 + half]], [Rf2])
                        tt('pool', scr[:, 9 + half, m_, :], ft[:], ft2[:], ALU.add, [Rf, Rf2], [R_scr[9 + half]])
                out_proj_tm(w_out, 8, lambda kk: scr[:, 9 + kk // 4, kk % 4, :], lambda kk: [R_scr[9 + kk // 4]])
                norm_block(1)
                for half in range(2):
                    def evq(m_, bk, bR, half=half):
                        act(scr[:, half, m_, :], bk[:, :], AF.Copy, r=[bR], w=[R_scr[half]], scale=1.0 / 16.0)
                    proj_fm(w_q, half * 512, 4, evq)
                for hd in range(4):
                    qs = hd // 2
                    for mt in range(2):
                        bk, bR = nb()
                        for dt_ in range(2):
                            mm(bk[:, :], kT[:, hd * 2 + dt_, mt * 128:(mt + 1) * 128], scr[:, qs, (hd % 2) * 2 + dt_, :],
                               dt_ == 0, dt_ == 1, [R_kT, R_scr[qs]], [bR])
                        act(scr[:, 2, mt, :], bk[:, :], AF.Exp, r=[bR], w=[R_scr[2]])
                    bk, bR = nb()
                    for mt in range(2):
                        mm(bk[:, :], ones_b[:], scr[:, 2, mt, :], mt == 0, mt == 1, [R_ident, R_scr[2]], [bR])
                    ft, Rf = nft()
                    S.op('dve', lambda e, ft=ft, bk=bk: e.reciprocal(out=ft[:], in_=bk[:, :]), r=[bR], w=[Rf])
                    for dt_ in range(2):
                        bk2, bR2 = nb()
                        for mt in range(2):
                            mm(bk2[:, :], vtok[:, mt, (hd * 2 + dt_) * 128:(hd * 2 + dt_ + 1) * 128], scr[:, 2, mt, :],
                               mt == 0, mt == 1, [R_v, R_scr[2]], [bR2])
                        kk = hd * 2 + dt_
                        tt('dve', scr[:, 3 + kk // 4, kk % 4, :], bk2[:, :], ft[:], ALU.mult, [bR2, Rf], [R_scr[3 + kk // 4]])
                out_proj_tm(w_xo, 8, lambda kk: scr[:, 3 + kk // 4, kk % 4, :], lambda kk: [R_scr[3 + kk // 4]])
                norm_block(2)
                c0 = 0
                while c0 < DFF:
                    ncol = min(512, DFF - c0)
                    nmt = ncol // 128
                    slg, Rg = wload(w_gate, 0, 8, c0, ncol)
                    slu, Ru = wload(w_up, 0, 8, c0, ncol)
                    for m_ in range(nmt):
                        mi = c0 // 128 + m_
                        bkg, bRg = nb()
                        for k in range(8):
                            mm(bkg[:, :], slg[:, k, m_ * 128:(m_ + 1) * 128], nT[:, k, :], k == 0, k == 7, [Rg] + R_nT, [bRg])
                        bku, bRu = nb()
                        for k in range(8):
                            mm(bku[:, :], slu[:, k, m_ * 128:(m_ + 1) * 128], nT[:, k, :], k == 0, k == 7, [Ru] + R_nT, [bRu])
                        ft, Rf = nft()
                        act(ft[:], bkg[:, :], AF.Silu, r=[bRg], w=[Rf])
                        tt('dve', scr[:, mi // 4, mi % 4, :], ft[:], bku[:, :], ALU.mult, [Rf, bRu], [R_scr[mi // 4]])
                    c0 += ncol
                out_proj_tm(w_down, 22, lambda kk: scr[:, kk // 4, kk % 4, :], lambda kk: [R_scr[kk // 4]])
                last = None
                for t_ in range(4):
                    act(junk[:], hb[t_][:], AF.Square, r=[R_h[t_]], w=[R_tmp], accum=ssum[:, 0:1])
                    act(rstd[:, 0:1], ssum[:, 0:1], AF.Sqrt, r=[R_tmp], w=[R_tmp], scale=1.0 / D, bias=epsb[:, 0:1])
                    S.op('dve', lambda e: e.reciprocal(out=rstd[:, 0:1], in_=rstd[:, 0:1]), r=[R_tmp], w=[R_tmp])
                    stt(hb[t_][:], hb[t_][:], rstd[:, 0:1], g4[:, 3, :], ALU.mult, ALU.mult, [R_h[t_], R_tmp, R_gb], [R_h[t_]])
                    last = S.op('sp', lambda e, t_=t_, tok0=tok0: e.dma_start(
                        out=out_d[tok0 + t_ * 128:tok0 + (t_ + 1) * 128, :], in_=hb[t_][:]), r=[R_h[t_]], chan=c_o[t_])
                    finals.append(last)
            S.barrier()
        S.emit(final_waits=finals)
    except _Stop:
        pass
    return nc


finals = []

_CACHE = {}


def _host_layout(inputs, core):
    b, hf = core // 2, core % 2
    rev = hf == 1
    f = lambda a: np.ascontiguousarray(a, dtype=np.float32)
    x = inputs["x"][b]
    xl = x[::-1] if rev else x
    dsel = [1, 0] if rev else [0, 1]
    lre = inputs["s5_lam_re"][0][dsel]
    lim = inputs["s5_lam_im"][0][dsel]
    ls = inputs["s5_log_step"][0][dsel]
    bre = inputs["s5_b_re"][0][dsel]
    bim = inputs["s5_b_im"][0][dsel]
    cre = inputs["s5_c_re"][0][dsel]
    cim = inputs["s5_c_im"][0][dsel]
    def layA_gp(a):
        t = a.reshape(2, 4, 8, 64)
        t = np.broadcast_to(t[:, :, :, None, :], (2, 4, 8, 16, 64))
        return t.transpose(2, 3, 0, 1, 4).reshape(128, 512)
    lsA = np.broadcast_to(ls[:, :, None], (2, 32, 64))
    def layA_b(a):
        t = a.reshape(2, 4, 8, 64, 16)
        return t.transpose(2, 4, 0, 1, 3).reshape(128, 512)
    A_in = np.stack([layA_gp(lre), layA_gp(lim), layA_gp(lsA), layA_b(bre), layA_b(bim)])
    def layB_gp(a):
        t = a.reshape(2, 16, 2, 64)
        t = np.broadcast_to(t[:, :, :, :, None], (2, 16, 2, 64, 16))
        return t.transpose(2, 3, 0, 1, 4).reshape(128, 512)
    def layB_c(a):
        t = a.reshape(2, 16, 2, 16, 64)
        return t.transpose(2, 4, 0, 1, 3).reshape(128, 512)
    B_in = np.stack([layB_gp(lre), layB_gp(lim), layB_gp(lsA), layB_c(cre), layB_c(cim)])
    Cc = np.concatenate([cre.transpose(3, 0, 1, 2).reshape(64, 1024), cim.transpose(3, 0, 1, 2).reshape(64, 1024)], axis=0)
    dcol = inputs["s5_d"][0].reshape(4, 128).T
    bmask = np.kron(np.eye(8, dtype=np.float32), np.ones((16, 16), np.float32))
    pm = ((np.arange(128) // 16) % 2)
    pmask = np.stack([(pm == 0), (pm == 1)], axis=1).astype(np.float32)
    ws = inputs["sgu_w"][0]
    sbias = inputs["sgu_bias"][0]
    if rev:
        ws = ws[:, ::-1, ::-1]
        sbias = sbias[:, ::-1]
    wsT = ws.transpose(2, 0, 1).reshape(128, 1024)
    gvec = np.stack([inputs["mix_norm_g"][0], inputs["xattn_norm_g"][0], inputs["mem_norm_g"],
                     inputs["ffn_norm_g"][0], inputs["final_norm_g"]])
    lnv = np.stack([inputs["sgu_ln_g"][0], inputs["sgu_ln_b"][0]])
    m = {
        "xl": f(xl), "mem": f(inputs["mem"][b]),
        "w_in": f(inputs["w_in"][0]), "w_glu": f(inputs["s5_w_glu"][0]),
        "w_pa": f(inputs["w_proj_a"][0]), "w_pb": f(inputs["w_proj_b"][0]), "w_out": f(inputs["w_out"][0]),
        "w_q": f(inputs["w_q"][0]), "w_k": f(inputs["w_k"][0]), "w_v": f(inputs["w_v"][0]), "w_xo": f(inputs["w_xo"][0]),
        "w_gate": f(inputs["w_gate"][0]), "w_up": f(inputs["w_up"][0]), "w_down": f(inputs["w_down"][0]),
        "gvec": f(gvec), "lnv": f(lnv), "wsT": f(wsT), "sbias": f(sbias.reshape(1, 1024)),
        "A_in": f(A_in), "B_in": f(B_in), "Cc": f(Cc), "dcol": f(dcol), "bmask": f(bmask), "pmask": f(pmask),
    }
    return m


def kernel(**inputs):
    inputs = {k: np.asarray(v) for k, v in inputs.items()}
    if "nc" not in _CACHE:
        finals.clear()
        _CACHE["nc"] = build_program()
    nc = _CACHE["nc"]
    in_maps = [_host_layout(inputs, c) for c in range(8)]
    res = run_bass_kernel_spmd(nc, in_maps, core_ids=list(range(8)))
    out = np.empty((4, 8192, D), np.float32)
    for c in range(8):
        b, hf = c // 2, c % 2
        o = np.asarray(res.results[c]["out"])
        if hf == 0:
            out[b, 0:4096] = o
        else:
            out[b, 4096:8192] = o[::-1]
    return out
```

```python
import numpy as np
from contextlib import ExitStack
import concourse.bass as bass
import concourse.mybir as mybir
from concourse.bass_utils import run_bass_kernel_spmd

F32 = mybir.dt.float32
BF16 = mybir.dt.bfloat16
AF = mybir.ActivationFunctionType
ALU = mybir.AluOpType

ENG = ['pe', 'act', 'dve', 'pool', 'sp']
NTOK = 4096
D = 1024
DFF = 2816
EPS = 1e-6
PI = float(np.pi)


class Res:
    __slots__ = ('name', 'last_w', 'readers', 'extra_w')

    def __init__(self, name=''):
        self.name = name
        self.last_w = None
        self.readers = {}
        self.extra_w = []


class Chan:
    def __init__(self, name, group=False):
        self.name = name
        self.group = group
        self.count = 0
        self.sem = None


class Ins:
    __slots__ = ('eng', 'fn', 'deps', 'signal', 'signo', 'chan', 'cval', 'idx')

    def __init__(self, eng, fn):
        self.eng = eng
        self.fn = fn
        self.deps = []
        self.signal = False
        self.signo = None
        self.chan = None
        self.cval = None


class Sched:
    def __init__(self, nc):
        self.nc = nc
        self.q = {e: [] for e in ENG}
        self.chans = []
        self.n = 0

    def chan(self, name, group=False):
        c = Chan(name, group)
        self.chans.append(c)
        return c

    def op(self, eng, fn, r=(), w=(), chan=None, extra=(), soft_w=()):
        ins = Ins(eng, fn)
        ins.idx = self.n
        self.n += 1
        deps = {}
        for res in r:
            if res.last_w is not None:
                deps[id(res.last_w)] = res.last_w
            for x in res.extra_w:
                deps[id(x)] = x
        for res in w:
            if res.last_w is not None:
                deps[id(res.last_w)] = res.last_w
            for rd in res.readers.values():
                deps[id(rd)] = rd
            for x in res.extra_w:
                deps[id(x)] = x
        for res in soft_w:
            if res.last_w is not None:
                deps[id(res.last_w)] = res.last_w
            for rd in res.readers.values():
                deps[id(rd)] = rd
            res.extra_w.append(ins)
        for d in extra:
            deps[id(d)] = d
        if chan is not None:
            ins.chan = chan
            chan.count += 1
            ins.cval = chan.count
        for d in deps.values():
            if d is ins:
                continue
            if d.chan is not None:
                ins.deps.append(d)
            elif d.eng != eng or eng != 'pe':
                if d.fn is None:
                    continue
                d.signal = True
                ins.deps.append(d)
        for res in r:
            key = eng if chan is None else ('dma', ins.idx)
            res.readers[key] = ins
        for res in w:
            res.last_w = ins
            res.readers = {}
        self.q[eng].append(ins)
        return ins

    def barrier(self):
        lasts = []
        for e in ENG:
            for ins in reversed(self.q[e]):
                if ins.chan is None and ins.fn is not None:
                    lasts.append(ins)
                    break
        dmas = {}
        for e in ENG:
            for ins in self.q[e]:
                if ins.chan is not None:
                    dmas[id(ins.chan)] = ins
        for e in ENG:
            self.op(e, None, extra=lasts + list(dmas.values()))

    def emit(self, final_waits=()):
        nc = self.nc
        with ExitStack() as st:
            esem = {}
            for e in ['pe', 'act', 'dve', 'pool']:
                esem[e] = st.enter_context(nc.semaphore('s_' + e))
            for c in self.chans:
                c.sem = st.enter_context(nc.semaphore('c_' + c.name))
            for e in ENG:
                k = 0
                for ins in self.q[e]:
                    if ins.signal and ins.chan is None:
                        k += 1
                        ins.signo = k
            block = st.enter_context(nc.Block())

            def replay(ename, eng):
                waited = {}
                q = self.q[ename]
                pending = None
                for ins in q:
                    for d in ins.deps:
                        if d.chan is not None:
                            c = d.chan
                            val = 16 * (c.count if c.group else d.cval)
                            key = ('c', id(c))
                            sem = c.sem
                        else:
                            val = d.signo
                            key = ('e', d.eng)
                            sem = esem[d.eng]
                        if waited.get(key, 0) >= val:
                            continue
                        waited[key] = val
                        eng.wait_ge(sem, val)
                    if ins.fn is None:
                        continue
                    bi = ins.fn(eng)
                    if ins.chan is not None:
                        bi.then_inc(ins.chan.sem, 16)
                    elif ins.signal:
                        bi.then_inc(esem[ename], 1)
                if ename == 'sp':
                    for d in final_waits:
                        c = d.chan
                        eng.wait_ge(c.sem, 16 * (c.count if c.group else d.cval))

            @block.tensor
            def _(e):
                replay('pe', e)

            @block.scalar
            def _(e):
                replay('act', e)

            @block.vector
            def _(e):
                replay('dve', e)

            @block.gpsimd
            def _(e):
                replay('pool', e)

            @block.sync
            def _(e):
                replay('sp', e)


class _Stop(Exception):
    pass


def build_program(stop=None, dumps=()):
    nc = bass.Bass("TRN2", target_bir_lowering=False)
    S = Sched(nc)
    REG = {}

    def din(name, shape):
        return nc.dram_tensor(name, list(shape), F32, kind="ExternalInput").ap()

    xl = din("xl", [8192, D])
    mem = din("mem", [256, D])
    w_in = din("w_in", [D, 4608])
    w_glu = din("w_glu", [512, 512])
    w_pa = din("w_pa", [512, D])
    w_pb = din("w_pb", [D, D])
    w_out = din("w_out", [D, D])
    w_q = din("w_q", [D, D])
    w_k = din("w_k", [D, D])
    w_v = din("w_v", [D, D])
    w_xo = din("w_xo", [D, D])
    w_gate = din("w_gate", [D, DFF])
    w_up = din("w_up", [D, DFF])
    w_down = din("w_down", [DFF, D])
    gvec = din("gvec", [5, D])
    lnv = din("lnv", [2, D])
    wsT_d = din("wsT", [128, 1024])
    sbias_d = din("sbias", [1, 1024])
    A_in = din("A_in", [5, 128, 512])
    B_in = din("B_in", [5, 128, 512])
    Cc_d = din("Cc", [128, 1024])
    dcol_d = din("dcol", [128, 4])
    bmask_d = din("bmask", [128, 128])
    pmask_d = din("pmask", [128, 2])
    out_d = nc.dram_tensor("out", [NTOK, D], F32, kind="ExternalOutput").ap()
    BFW = {}
    for wap in (w_in, w_glu, w_pa, w_pb, w_out, w_q, w_xo, w_gate, w_up, w_down):
        BFW[wap.name] = (wap, nc.dram_tensor(wap.name + "_bf", list(wap.shape), BF16, kind="Internal").ap())
    R_bfw = Res("bfw")

    try:
      with ExitStack() as top:
        def sb(name, shape, dt, ctx=top):
            t = ctx.enter_context(nc.sbuf_tensor(name, list(shape), dt))
            REG[name] = (t, list(shape), dt)
            return t

        def checkpoint(name):
            if stop != name:
                return
            S.barrier()
            cd = S.chan("dump", group=True)
            lastd = []
            for dn in dumps:
                t, shape, dt = REG[dn]
                flat = int(np.prod(shape[1:]))
                dd = nc.dram_tensor("dbg_" + dn, [shape[0], flat], dt, kind="ExternalOutput").ap()
                src_ap = t[:]
                if len(shape) > 2:
                    letters = "abcdefg"[:len(shape) - 1]
                    src_ap = src_ap.rearrange("p " + " ".join(letters) + " -> p (" + " ".join(letters) + ")")
                lastd.append(S.op('sp', lambda e, dd=dd, src_ap=src_ap: e.dma_start(out=dd[:, :], in_=src_ap), chan=cd))
            S.emit(final_waits=lastd)
            raise _Stop()

        banks = [top.enter_context(nc.psum_tensor(f"bank{i}", [128, 512], F32)) for i in range(8)]
        bankR = [Res(f"bank{i}") for i in range(8)]
        bstate = {'i': 0}

        def nb():
            i = bstate['i']
            bstate['i'] = (i + 1) % 8
            return banks[i], bankR[i]

        ident = sb("ident", [128, 128], BF16)
        identf = sb("identf", [128, 128], F32)
        ones_b = sb("ones_b", [128, 128], BF16)
        kT = sb("kT", [128, 8, 256], BF16)
        vtok = sb("vtok", [128, 2, D], BF16)
        uT_own = sb("uT_own", [128, 4, NTOK], BF16)
        R_ident, R_gb, R_ln, R_ws, R_sbias = Res(), Res(), Res(), Res(), Res()
        R_kT, R_v = Res(), Res()
        R_uo = [Res(f"uo{g}") for g in range(4)]
        cur = {'name': "setup0"}

        def new_setup_chan(name):
            cur.clear()
            cur['name'] = name

        def cur_chan(eng):
            if eng not in cur:
                cur[eng] = S.chan(cur['name'] + "_" + eng, group=True)
            return cur[eng]

        def setup_load(eng, out_ap, in_ap, res, fresh=True):
            if fresh:
                ins = S.op(eng, lambda e: e.dma_start(out=out_ap, in_=in_ap), chan=cur_chan(eng))
                res.extra_w.append(ins)
            else:
                S.op(eng, lambda e: e.dma_start(out=out_ap, in_=in_ap), soft_w=[res], chan=cur_chan(eng))

        S.op('pool', lambda e: e.memset(ident[:], 0.0), w=[R_ident])
        S.op('pool', lambda e: e.affine_select(out=ident[:], in_=ident[:], pattern=[[-1, 128]],
                                               compare_op=ALU.not_equal, fill=1.0, base=0,
                                               channel_multiplier=1), r=[R_ident], w=[R_ident])
        S.op('pool', lambda e: e.memset(identf[:], 0.0), w=[R_ident])
        S.op('pool', lambda e: e.affine_select(out=identf[:], in_=identf[:], pattern=[[-1, 128]],
                                               compare_op=ALU.not_equal, fill=1.0, base=0,
                                               channel_multiplier=1), r=[R_ident], w=[R_ident])
        S.op('pool', lambda e: e.memset(ones_b[:], 1.0), w=[R_ident])

        def act(out, in_, func, r, w, scale=1.0, bias=0.0, accum=None):
            def f(e):
                kw = {}
                if accum is not None:
                    kw['accum_out'] = accum
                return e.activation(out=out, in_=in_, func=func, scale=scale, bias=bias, **kw)
            return S.op('act', f, r=r, w=w)

        def tt(eng, out, a, b, op, r, w):
            return S.op(eng, lambda e: e.tensor_tensor(out=out, in0=a, in1=b, op=op), r=r, w=w)

        def ts(eng, out, a, s1, s2, op0, op1, r, w):
            if s2 is None:
                return S.op(eng, lambda e: e.tensor_scalar(out=out, in0=a, scalar1=s1, scalar2=None, op0=op0), r=r, w=w)
            return S.op(eng, lambda e: e.tensor_scalar(out=out, in0=a, scalar1=s1, scalar2=s2, op0=op0, op1=op1), r=r, w=w)

        def stt(out, a, s, b, op0, op1, r, w):
            return S.op('dve', lambda e: e.scalar_tensor_tensor(out=out, in0=a, scalar=s, in1=b, op0=op0, op1=op1), r=r, w=w)

        def mm(out, lhsT, rhs, start, stop, r, w, tp=None):
            if tp is None:
                return S.op('pe', lambda e: e.matmul(out, lhsT=lhsT, rhs=rhs, start=start, stop=stop), r=r, w=w)
            return S.op('pe', lambda e: e.matmul(out, lhsT=lhsT, rhs=rhs, start=start, stop=stop, tile_position=tp), r=r, w=w)

        def tcopy(eng, out, in_, r, w):
            if eng == 'act':
                return S.op(eng, lambda e: e.activation(out=out, in_=in_, func=AF.Copy), r=r, w=w)
            return S.op(eng, lambda e: e.tensor_copy(out=out, in_=in_), r=r, w=w)

        def rmsnorm_to(out_bf, x_ap, g_ap, junk, ssum, rstd, r, w, Rtmp):
            act(junk, x_ap, AF.Square, r=r, w=[Rtmp], accum=ssum)
            act(rstd, ssum, AF.Sqrt, r=[Rtmp], w=[Rtmp], scale=1.0 / D, bias=epsb[:, 0:1])
            S.op('dve', lambda e: e.reciprocal(out=rstd, in_=rstd), r=[Rtmp], w=[Rtmp])
            return stt(out_bf, x_ap, rstd, g_ap, ALU.mult, ALU.mult, r=r + [Rtmp, R_gb], w=w)

        epsb = sb("epsb", [128, 1], F32)
        S.op('pool', lambda e: e.memset(epsb[:], EPS), w=[R_ident])

        def transpose_to(nT, col0, src_bf, r, w):
            bk, bR = nb()
            bkb = bk[:].bitcast(BF16)
            for k in range(8):
                S.op('pe', lambda e, k=k: e.transpose(out=bkb[:, k * 128:(k + 1) * 128],
                                                        in_=src_bf[:, k * 128:(k + 1) * 128],
                                                        identity=ident[:]),
                     r=r + [R_ident], w=[bR])
            tcopy('act', nT[:, :, col0:col0 + 128], bkb[:, :].rearrange("p (k c) -> p k c", k=8), r=[bR], w=w)

        NSLOT = 4
        with ExitStack() as ph:
            Wd = sb("Wd", [128, 2, 4, 2, 8, 128], BF16, ph)
            Wo = sb("Wo", [128, 2, 2, 16, 8, 32], BF16, ph)
            Kt = sb("Kt", [128, 4, 15, 128], BF16, ph)
            QT = sb("QT", [128, 3, 9, 3, 32], F32, ph)
            q4096 = sb("q4096", [128, 3, 32], F32, ph)
            R_Wd, R_Wo, R_Kt, R_QT = Res(), Res(), Res(), Res()
            S.op('pool', lambda e: e.memset(QT[:], 0.0), w=[R_QT])
            R_ut = [Res(f"ut{g}") for g in range(4)]

            with ExitStack() as ps_:
                arr_cache = {}

                def arr(name, n=512):
                    if name not in arr_cache:
                        arr_cache[name] = sb("S_" + name, [128, n], F32, ps_)
                    return arr_cache[name]
                Rs = Res("S")

                def lam_derive(src, pre, levels=False, fresh=True, extra_loads=()):
                    lre, lim, ls = arr("lre"), arr("lim"), arr("ls")
                    for i, t in enumerate([lre, lim, ls]):
                        setup_load('sp', t[:], src[i, :, :], Rs, fresh)
                    for (t, i) in extra_loads:
                        setup_load('sp', t[:], src[i, :, :], Rs, fresh)
                    R1 = ([Rs], [Rs])
                    x, h, dt_ = arr("x"), arr("h"), arr("dt")
                    ts('dve', x[:], ls[:], -1.0 / 128, None, ALU.mult, None, *R1)
                    ts('dve', h[:], x[:], 1.0 / 5, 1.0, ALU.mult, ALU.add, *R1)
                    for k in (4, 3, 2):
                        tt('dve', h[:], h[:], x[:], ALU.mult, *R1)
                        ts('dve', h[:], h[:], 1.0 / k, 1.0, ALU.mult, ALU.add, *R1)
                    tt('dve', h[:], h[:], x[:], ALU.mult, *R1)
                    for _ in range(7):
                        stt(h[:], h[:], 2.0, h[:], ALU.add, ALU.mult, *R1)
                    ts('dve', h[:], h[:], 1.0, None, ALU.add, None, *R1)
                    S.op('dve', lambda e: e.reciprocal(out=dt_[:], in_=h[:]), r=[Rs], w=[Rs])
                    a, b = arr("za"), arr("zb")
                    tt('dve', a[:], lre[:], dt_[:], ALU.mult, *R1)
                    ts('dve', a[:], a[:], 1.0 / 64, None, ALU.mult, None, *R1)
                    tt('dve', b[:], lim[:], dt_[:], ALU.mult, *R1)
                    ts('dve', b[:], b[:], 1.0 / 64, None, ALU.mult, None, *R1)
                    hr, hi, u1, u2, u3 = arr("hr"), arr("hi"), arr("u1"), arr("u2"), arr("u3")
                    ts('dve', hr[:], a[:], 1.0 / 7, 1.0, ALU.mult, ALU.add, *R1)
                    ts('dve', hi[:], b[:], 1.0 / 7, None, ALU.mult, None, *R1)

                    def zmul_h():
                        tt('dve', u1[:], a[:], hr[:], ALU.mult, *R1)
                        tt('dve', u2[:], b[:], hi[:], ALU.mult, *R1)
                        tt('dve', u1[:], u1[:], u2[:], ALU.subtract, *R1)
                        tt('dve', u3[:], a[:], hi[:], ALU.mult, *R1)
                        tt('dve', u2[:], b[:], hr[:], ALU.mult, *R1)
                        tt('dve', u3[:], u3[:], u2[:], ALU.add, *R1)
                    for k in (6, 5, 4, 3, 2):
                        zmul_h()
                        ts('dve', hr[:], u1[:], 1.0 / k, 1.0, ALU.mult, ALU.add, *R1)
                        ts('dve', hi[:], u3[:], 1.0 / k, None, ALU.mult, None, *R1)
                    zmul_h()
                    wr, wi = u1, u3

                    def square():
                        ts('dve', u2[:], wr[:], 2.0, None, ALU.add, None, *R1)
                        tt('dve', hr[:], wr[:], u2[:], ALU.mult, *R1)
                        tt('dve', hi[:], wi[:], wi[:], ALU.mult, *R1)
                        tt('dve', u2[:], u2[:], wr[:], ALU.add, *R1)
                        tt('dve', wr[:], hr[:], hi[:], ALU.subtract, *R1)
                        tt('dve', wi[:], wi[:], u2[:], ALU.mult, *R1)
                    for _ in range(6):
                        square()
                    lbr, lbi, lm1 = arr("lbr"), arr("lbi"), arr("lm1")
                    ts('dve', lbr[:], wr[:], 1.0, None, ALU.add, None, *R1)
                    tcopy('dve', lbi[:], wi[:], *R1)
                    tcopy('dve', lm1[:], wr[:], *R1)
                    if levels:
                        wrc = wr[:].rearrange("p (g c) -> p g c", c=16)[:, :, 0]
                        wic = wi[:].rearrange("p (g c) -> p g c", c=16)[:, :, 0]
                        for lvl in range(4):
                            for _ in range(3):
                                square()
                            if lvl < 3:
                                ts('dve', QT[:, lvl, 1, 0, :], wrc, 1.0, None, ALU.add, None, [Rs], [Rs, R_QT])
                                tcopy('dve', QT[:, lvl, 1, 1, :], wic, [Rs], [Rs, R_QT])
                            else:
                                ts('dve', q4096[:, 0, :], wrc, 1.0, None, ALU.add, None, [Rs], [Rs, R_QT])
                                tcopy('dve', q4096[:, 1, :], wic, [Rs], [Rs, R_QT])
                    return lre, lim, lbr, lbi, lm1

                def cmul(o_re, o_im, a_re, a_im, b_re, b_im, t1, t2):
                    tt('dve', t1, a_im, b_im, ALU.mult, [Rs], [Rs])
                    tt('dve', t2, a_im, b_re, ALU.mult, [Rs], [Rs])
                    tt('dve', o_re, a_re, b_re, ALU.mult, [Rs], [Rs])
                    tt('dve', o_re, o_re, t1, ALU.subtract, [Rs], [Rs])
                    tt('dve', o_im, a_re, b_im, ALU.mult, [Rs], [Rs])
                    tt('dve', o_im, o_im, t2, ALU.add, [Rs], [Rs])

                checkpoint('S0')
                lre, lim, lbr, lbi, lm1 = lam_derive(A_in, "A")
                checkpoint('S1')
                bre, bim = arr("Abre"), arr("Abim")
                setup_load('sp', bre[:], A_in[3, :, :], Rs)
                setup_load('sp', bim[:], A_in[4, :, :], Rs)
                bmask = sb("bmask_s", [128, 128], F32, ps_)
                pmask = sb("pmask_s", [128, 2], F32, ps_)
                dcol = sb("dcol_s", [128, 4], F32, ps_)
                Cc = sb("Cc_s", [128, 1024], F32, ps_)
                setup_load('sp', bmask[:], bmask_d[:, :], Rs)
                setup_load('sp', pmask[:], pmask_d[:, :], Rs)
                setup_load('sp', dcol[:], dcol_d[:, :], Rs)
                setup_load('sp', Cc[:], Cc_d[:, :], Rs)
                act(Cc[64:128, :], Cc[64:128, :], AF.Copy, r=[Rs], w=[Rs], scale=-1.0)
                t1, t2, t3, t4 = arr("t1"), arr("t2"), arr("t3"), arr("t4")
                den = arr("den")
                tt('dve', den[:], lre[:], lre[:], ALU.mult, [Rs], [Rs])
                tt('dve', t1[:], lim[:], lim[:], ALU.mult, [Rs], [Rs])
                tt('dve', den[:], den[:], t1[:], ALU.add, [Rs], [Rs])
                S.op('dve', lambda e: e.reciprocal(out=den[:], in_=den[:]), r=[Rs], w=[Rs])
                cre_, cim_ = arr("coefre"), arr("coefim")
                tt('dve', t1[:], lm1[:], lre[:], ALU.mult, [Rs], [Rs])
                tt('dve', t2[:], lbi[:], lim[:], ALU.mult, [Rs], [Rs])
                tt('dve', t1[:], t1[:], t2[:], ALU.add, [Rs], [Rs])
                tt('dve', cre_[:], t1[:], den[:], ALU.mult, [Rs], [Rs])
                tt('dve', t1[:], lbi[:], lre[:], ALU.mult, [Rs], [Rs])
                tt('dve', t2[:], lm1[:], lim[:], ALU.mult, [Rs], [Rs])
                tt('dve', t1[:], t1[:], t2[:], ALU.subtract, [Rs], [Rs])
                tt('dve', cim_[:], t1[:], den[:], ALU.mult, [Rs], [Rs])
                wre = [arr("wre0"), arr("wre1")]
                wim = [arr("wim0"), arr("wim1")]
                cmul(wre[0][:], wim[0][:], cre_[:], cim_[:], bre[:], bim[:], t1[:], t2[:])
                tstk = sb("tstk", [128, 128], F32, ps_)
                tT = sb("tT", [128, 128], F32, ps_)
                kacc = sb("kacc", [128, 4, 128], F32, ps_)
                for n in range(8):
                    cur_re, cur_im = wre[n % 2], wim[n % 2]
                    v_re = cur_re[:].rearrange("p (d g q) -> p d g q", d=2, g=4)
                    v_im = cur_im[:].rearrange("p (d g q) -> p d g q", d=2, g=4)
                    for d_ in range(2):
                        j = 7 - n if d_ == 0 else n
                        for ri, v in ((0, v_re), (1, v_im)):
                            for g2 in range(2):
                                ts('dve', Wd[:, d_, :, ri, j, g2 * 64:(g2 + 1) * 64], v[:, d_, :, :],
                                   pmask[:, g2:g2 + 1], None, ALU.mult, None, [Rs], [Rs, R_Wd])
                        for gt in range(4):
                            tcopy('dve', tstk[:, 0:64], v_re[:, d_, gt, :], [Rs], [Rs])
                            tcopy('dve', tstk[:, 64:128], v_im[:, d_, gt, :], [Rs], [Rs])
                            bk, bR = nb()
                            S.op('pe', lambda e, bk=bk: e.transpose(out=bk[:, 0:128], in_=tstk[:], identity=identf[:]),
                                 r=[Rs, R_ident], w=[bR])
                            tcopy('act', tT[:], bk[:, 0:128], [bR], [Rs])
                            bk2, bR2 = nb()
                            cc = Cc[:, (d_ * 4 + gt) * 128:(d_ * 4 + gt + 1) * 128]
                            mm(bk2[:, 0:128], tT[:], cc, True, True, [Rs], [bR2])
                            if n == 0:
                                if d_ == 0:
                                    tt('dve', kacc[:, gt, :], bk2[:, 0:128], bmask[:], ALU.mult, [bR2, Rs], [Rs])
                                    stt(kacc[:, gt, :], identf[:], dcol[:, gt:gt + 1], kacc[:, gt, :], ALU.mult, ALU.add,
                                        [Rs, R_ident], [Rs])
                                else:
                                    tt('dve', tstk[:], bk2[:, 0:128], bmask[:], ALU.mult, [bR2, Rs], [Rs])
                                    tt('dve', Kt[:, gt, 0, :], tstk[:], kacc[:, gt, :], ALU.add, [Rs], [Rs, R_Kt])
                            else:
                                tt('dve', Kt[:, gt, d_ * 7 + n, :], bk2[:, 0:128], bmask[:], ALU.mult, [bR2, Rs], [Rs, R_Kt])
                    if n < 7:
                        nxt_re, nxt_im = wre[(n + 1) % 2], wim[(n + 1) % 2]
                        cmul(nxt_re[:], nxt_im[:], lbr[:], lbi[:], cur_re[:], cur_im[:], t1[:], t2[:])

                checkpoint('S2')
                new_setup_chan("setupB")
                creB, cimB = arr("Abre"), arr("Abim")
                lreB, limB, lbrB, lbiB, _ = lam_derive(B_in, "B", levels=True, fresh=False,
                                                       extra_loads=[(creB, 3), (cimB, 4)])
                S.op('pool', lambda e: e.memset(Wo[:], 0.0), w=[R_Wo])
                pw_re = [arr("wre0"), arr("wre1")]
                pw_im = [arr("wim0"), arr("wim1")]
                tcopy('dve', pw_re[1][:], lbrB[:], [Rs], [Rs])
                tcopy('dve', pw_im[1][:], lbiB[:], [Rs], [Rs])
                for n in range(1, 9):
                    pr, pi_ = pw_re[n % 2], pw_im[n % 2]
                    cmul(t3[:], t4[:], creB[:], cimB[:], pr[:], pi_[:], t1[:], t2[:])
                    ts('dve', t4[:], t4[:], -1.0, None, ALU.mult, None, [Rs], [Rs])
                    for d_ in range(2):
                        tl = n - 1 if d_ == 0 else 8 - n
                        for ri, src in ((0, t3), (1, t4)):
                            sv = src[:].rearrange("p (d g c) -> p d g c", d=2, g=16)
                            for g2 in range(2):
                                tcopy('dve', Wo[g2 * 64:(g2 + 1) * 64, ri, d_, :, tl, g2 * 16:(g2 + 1) * 16],
                                      sv[g2 * 64:(g2 + 1) * 64, d_, :, :], [Rs], [Rs, R_Wo])
                    if n < 8:
                        cmul(pw_re[(n + 1) % 2][:], pw_im[(n + 1) % 2][:], lbrB[:], lbiB[:], pr[:], pi_[:], t1[:], t2[:])
                checkpoint('S3')
                for lvl in range(3):
                    for m in range(2, 9):
                        cmul(QT[:, lvl, m, 0, :], QT[:, lvl, m, 1, :], QT[:, lvl, 1, 0, :], QT[:, lvl, 1, 1, :],
                             QT[:, lvl, m - 1, 0, :], QT[:, lvl, m - 1, 1, :], t1[:, 0:32], t2[:, 0:32])
                for lvl in range(3):
                    ts('dve', QT[:, lvl, 1:9, 2, :], QT[:, lvl, 1:9, 1, :], -1.0, None, ALU.mult, None, [Rs], [Rs, R_QT])
                ts('dve', q4096[:, 2, :], q4096[:, 1, :], -1.0, None, ALU.mult, None, [Rs], [Rs, R_QT])

                checkpoint('S')
                S.barrier()
            with ExitStack() as pm_:
                new_setup_chan("setupM")
                Rs = Res("M")
                gmem = sb("gmem", [128, D], F32, pm_)
                setup_load('sp', gmem[:], gvec[2:3, :].partition_broadcast(128), R_gb)
                memt = sb("memt", [128, 2, D], F32, pm_)
                memb = sb("memb", [128, 2, D], BF16, pm_)
                memT = sb("memT", [128, 8, 256], BF16, pm_)
                wkv = sb("wkv", [128, 8, 512], BF16, pm_)
                junk = sb("junkS", [128, D], BF16, pm_)
                ssum = sb("ssumS", [128, 1], F32, pm_)
                rstd = sb("rstdS", [128, 1], F32, pm_)
                setup_load('sp', memt[:], mem.rearrange("(t p) d -> p t d", p=128), Rs)
                for t_ in range(2):
                    rmsnorm_to(memb[:, t_, :], memt[:, t_, :], gmem[:], junk[:], ssum[:], rstd[:], [Rs], [Rs], Rs)
                    transpose_to(memT, t_ * 128, memb[:, t_, :], [Rs], [Rs])
                c_kv = S.chan("kv")
                for half in range(2):
                    S.op('pool', lambda e, half=half: e.dma_start(
                        out=wkv[:], in_=w_k[:, half * 512:(half + 1) * 512].rearrange("(k p) c -> p k c", p=128)),
                        w=[Rs], chan=c_kv)
                    for m_ in range(4):
                        bk, bR = nb()
                        for k in range(8):
                            mm(bk[:, 0:256], wkv[:, k, m_ * 128:(m_ + 1) * 128], memT[:, k, :], k == 0, k == 7, [Rs], [bR])
                        tcopy('act', kT[:, half * 4 + m_, :], bk[:, 0:256], [bR], [R_kT])
                for half in range(2):
                    S.op('pool', lambda e, half=half: e.dma_start(
                        out=wkv[:], in_=w_v[:, half * 512:(half + 1) * 512].rearrange("(k p) c -> p k c", p=128)),
                        w=[Rs], chan=c_kv)
                    for t_ in range(2):
                        bk, bR = nb()
                        for k in range(8):
                            mm(bk[:, :], memT[:, k, t_ * 128:(t_ + 1) * 128], wkv[:, k, :], k == 0, k == 7, [Rs], [bR])
                        tcopy('act', vtok[:, t_, half * 512:(half + 1) * 512], bk[:, :], [bR], [R_v])
                checkpoint('M')
                S.barrier()

            uT_oth = sb("uT_oth", [128, 4, NTOK], BF16, ph)
            with ExitStack() as pa_:
                new_setup_chan("setupA")
                gmix = sb("gmix", [128, D], F32, pa_)
                setup_load('sp', gmix[:], gvec[0:1, :].partition_broadcast(128), R_gb)
                xs = [sb(f"xsA{i}", [128, D], F32, pa_) for i in range(2)]
                R_xs = [Res() for _ in range(2)]
                c_xs = [S.chan(f"xsA{i}") for i in range(2)]
                nb_ = [sb(f"nbA{i}", [128, D], BF16, pa_) for i in range(2)]
                R_nb = [Res() for _ in range(2)]
                nT = [sb(f"nTA{i}", [128, 8, 512], BF16, pa_) for i in range(2)]
                R_nT = [Res() for _ in range(2)]
                wxa = sb("wxa", [128, 8, 512], BF16, pa_)
                R_wxa = Res()
                junk = sb("junkA", [128, D], BF16, pa_)
                ssum = sb("ssumA", [128, 1], F32, pa_)
                rstd = sb("rstdA", [128, 1], F32, pa_)
                R_tmp = Res()
                c_wxa = S.chan("wxa")
                S.op('pool', lambda e: e.dma_start(out=wxa[:], in_=w_in[:, 0:512].rearrange("(k p) c -> p k c", p=128)),
                     w=[R_wxa], chan=c_wxa)
                c_cast = S.chan("wcast", group=True)
                for nm, (wsrc, wdst) in BFW.items():
                    rows, cols = wsrc.shape
                    for r0_ in range(0, rows, 1024):
                        nr = min(1024, rows - r0_)
                        for c0_ in range(0, cols, 512):
                            ncl = min(512, cols - c0_)
                            ins_ = S.op('pool', lambda e, wsrc=wsrc, wdst=wdst, r0_=r0_, nr=nr, c0_=c0_, ncl=ncl: e.dma_start(
                                out=wdst[r0_:r0_ + nr, c0_:c0_ + ncl], in_=wsrc[r0_:r0_ + nr, c0_:c0_ + ncl]), chan=c_cast)
                            R_bfw.extra_w.append(ins_)
                for blk in range(16):
                    nTb, RnT = nT[blk % 2], R_nT[blk % 2]
                    for t_ in range(4):
                        i = (blk * 4 + t_)
                        tok0 = blk * 512 + t_ * 128
                        x_, Rx, cx = xs[i % 2], R_xs[i % 2], c_xs[i % 2]
                        S.op('sp', lambda e, x_=x_, tok0=tok0: e.dma_start(out=x_[:], in_=xl[tok0:tok0 + 128, :]),
                             w=[Rx], chan=cx)
                        nbb, Rnb = nb_[i % 2], R_nb[i % 2]
                        rmsnorm_to(nbb[:], x_[:], gmix[:], junk[:], ssum[:], rstd[:], [Rx], [Rnb], R_tmp)
                        transpose_to(nTb, t_ * 128, nbb[:], [Rnb], [RnT])
                    for m_ in range(4):
                        bk, bR = nb()
                        for k in range(8):
                            mm(bk[:, :], wxa[:, k, m_ * 128:(m_ + 1) * 128], nTb[:, k, :], k == 0, k == 7, [R_wxa, RnT], [bR])
                        if blk < 8:
                            tcopy('act' if m_ % 2 == 0 else 'dve', uT_own[:, m_, blk * 512:(blk + 1) * 512], bk[:, :], [bR], [R_uo[m_]])
                        else:
                            tcopy('act' if m_ % 2 == 0 else 'dve', uT_oth[:, m_, (blk - 8) * 512:(blk - 7) * 512], bk[:, :], [bR], [R_ut[m_]])
                checkpoint('A')
                S.barrier()

            with ExitStack() as pb_:
                Dt = [sb(f"Dt{i}", [128, 2, 1024], F32, pb_) for i in range(2)]
                R_D = [Res() for _ in range(2)]
                Sb = sb("Sb", [128, 8, 2, 512], BF16, pb_)
                R_Sb = [Res() for _ in range(8)]
                ytmp = sb("ytmp", [128, 4, 512], BF16, pb_)
                R_yt = Res()
                ykeep = sb("ykeep", [128, 4, 512], BF16, pb_)
                R_yk = Res()
                tile_i = 0
                for gt in range(4):
                    uo = uT_own[:, gt, :].rearrange("p (k t) -> p k t", t=8)
                    ut = uT_oth[:, gt, :].rearrange("p (k t) -> p k t", t=8)
                    for d_ in range(2):
                        for gpl in range(4):
                            gp = gt * 4 + gpl
                            Dtile, RD = Dt[tile_i % 2], R_D[tile_i % 2]
                            tile_i += 1
                            nk = 512 if d_ == 0 else 1024
                            halves = [(uo, R_uo[gt], 0)] + ([(ut, R_ut[gt], 512)] if d_ == 1 else [])
                            for (uv, Ru, c0) in halves:
                                for ri in range(2):
                                    bk, bR = nb()
                                    for j in range(8):
                                        mm(bk[:, :], Wd[32 * gpl:32 * gpl + 32, d_, gt, ri, j, :], uv[32 * gpl:32 * gpl + 32, :, j],
                                           j == 0, j == 7, [R_Wd, Ru], [bR], tp=(32 * gpl, 0))
                                    tcopy('act', Dtile[:, ri, c0:c0 + 512], bk[:, :], [bR], [RD])
                            qi = d_ * 16 + gp

                            def scal(lvl, m, which, qi=qi):
                                if lvl == 3:
                                    return q4096[:, which, qi:qi + 1]
                                return QT[:, lvl, m, which, qi:qi + 1]

                            def cmac(dst_c, src_c, lvl, m, cnt, st_dst, st_src, Dtile=Dtile, RD=RD):
                                def view(ri, c):
                                    start, count, stride = c
                                    if count == 1 or stride == 1:
                                        return Dtile[:, ri, start:start + count]
                                    rr_ = max(0, start + count * stride - 1024)
                                    b0 = start - rr_
                                    return Dtile[:, ri, b0:b0 + count * stride].rearrange("p (c s) -> p c s", s=stride)[:, :, rr_]
                                dre, dim_ = view(0, dst_c), view(1, dst_c)
                                sre, sim = view(0, src_c), view(1, src_c)
                                rr = [RD, R_QT]
                                stt(dre, sre, scal(lvl, m, 0), dre, ALU.mult, ALU.add, rr, [RD])
                                stt(dre, sim, scal(lvl, m, 2), dre, ALU.mult, ALU.add, rr, [RD])
                                stt(dim_, sim, scal(lvl, m, 0), dim_, ALU.mult, ALU.add, rr, [RD])
                                stt(dim_, sre, scal(lvl, m, 1), dim_, ALU.mult, ALU.add, rr, [RD])

                            def cscan(off, st, n, lvl, rev):
                                if n <= 8:
                                    order = range(1, n) if not rev else range(n - 2, -1, -1)
                                    for jx in order:
                                        prev = jx - 1 if not rev else jx + 1
                                        cmac((off + jx * st, 1, 1), (off + prev * st, 1, 1), lvl, 1, 1, 0, 0)
                                    return
                                M = n // 8
                                order = range(1, 8) if not rev else range(6, -1, -1)
                                for i_ in order:
                                    prev = i_ - 1 if not rev else i_ + 1
                                    cmac((off + i_ * st, M, 8 * st), (off + prev * st, M, 8 * st), lvl, 1, M, 0, 0)
                                endpos = 7 if not rev else 0
                                cscan(off + endpos * st, 8 * st, M, lvl + 1, rev)
                                if not rev:
                                    for i_ in range(7):
                                        cmac((off + (8 + i_) * st, M - 1, 8 * st), (off + 7 * st, M - 1, 8 * st), lvl, i_ + 1, M - 1, 0, 0)
                                else:
                                    for i_ in range(1, 8):
                                        cmac((off + i_ * st, M - 1, 8 * st), (off + 8 * st, M - 1, 8 * st), lvl, 8 - i_, M - 1, 0, 0)

                            cscan(0, 1, nk, 0, d_ == 1)
                            sidx = d_ * 4 + gpl
                            for ri in range(2):
                                if d_ == 0:
                                    S.op('dve', lambda e, sidx=sidx, ri=ri: e.memset(Sb[:, sidx, ri, 0:1], 0.0), w=[R_Sb[sidx]])
                                    tcopy('act', Sb[:, sidx, ri, 1:512], Dtile[:, ri, 0:511], [RD], [R_Sb[sidx]])
                                else:
                                    tcopy('act', Sb[:, sidx, ri, :], Dtile[:, ri, 1:513], [RD], [R_Sb[sidx]])
                    for hp in range(2):
                        bks = [nb() for _ in range(4)]
                        for ti in range(4):
                            tl = hp * 4 + ti
                            bk, bR = bks[ti]
                            first = True
                            mm(bk[:, :], Kt[:, gt, 0, :], uo[:, :, tl], True, False, [R_Kt, R_uo[gt]], [bR])
                            for tau in range(1, tl + 1):
                                mm(bk[:, :], Kt[:, gt, tau, :], uo[:, :, tl - tau], False, False, [R_Kt, R_uo[gt]], [bR])
                            for tau in range(1, 8 - tl):
                                mm(bk[:, :], Kt[:, gt, 7 + tau, :], uo[:, :, tl + tau], False, False, [R_Kt, R_uo[gt]], [bR])
                            for gpl in range(4):
                                for d_ in range(2):
                                    gp = gt * 4 + gpl
                                    sidx = d_ * 4 + gpl
                                    for ri in range(2):
                                        mm(bk[32 * gpl:32 * gpl + 32, :], Wo[:, ri, d_, gp, tl, :], Sb[:, sidx, ri, :],
                                           False, d_ == 1 and ri == 1, [R_Wo, R_Sb[sidx]], [bR], tp=(0, 32 * gpl))
                            act(ytmp[:, ti, :], bk[:, :], AF.Gelu_apprx_tanh, r=[bR], w=[R_yt])
                        if hp == 1:
                            pass
                        if hp == 0:
                            tcopy('act', ykeep[:], ytmp[:], [R_yt], [R_yk])
                        else:
                            for ti in range(4):
                                tcopy('act', uo[:, :, ti], ykeep[:, ti, :], [R_yk], [R_uo[gt]])
                                tcopy('act', uo[:, :, 4 + ti], ytmp[:, ti, :], [R_yt], [R_uo[gt]])
                checkpoint('B')
                S.barrier()

        with ExitStack() as pc_:
            hb = [sb(f"hb{i}", [128, D], F32, pc_) for i in range(4)]
            R_h = [Res() for _ in range(4)]
            c_h = [S.chan(f"h{i}") for i in range(4)]
            c_o = [S.chan(f"o{i}") for i in range(4)]
            nbt = [sb(f"nbC{i}", [128, D], BF16, pc_) for i in range(2)]
            R_nbt = [Res() for _ in range(2)]
            nT = sb("nTC", [128, 8, 512], BF16, pc_)
            R_nT = [Res() for _ in range(4)]
            wsl = [sb(f"wsl{i}", [128, 8, 512], BF16, pc_) for i in range(NSLOT)]
            R_ws_ = [Res() for _ in range(NSLOT)]
            c_ws = [S.chan(f"ws{i}") for i in range(NSLOT)]
            wst = {'i': 0}
            scr = sb("scr", [128, 12, 4, 512], BF16, pc_)
            R_scr = [Res() for _ in range(12)]
            zvf = [sb(f"zvf{i}", [128, D], F32, pc_) for i in range(2)]
            R_zvf = [Res() for _ in range(2)]
            zvn = [sb(f"zvn{i}", [128, D], BF16, pc_) for i in range(2)]
            R_zvn = [Res() for _ in range(2)]
            junk = sb("junkC", [128, D], BF16, pc_)
            new_setup_chan("setupC")
            g4 = sb("g4", [128, 4, D], F32, pc_)
            for i_, gi_ in enumerate((0, 1, 3, 4)):
                setup_load('sp', g4[:, i_, :], gvec[gi_:gi_ + 1, :].partition_broadcast(128), R_gb)
            ln_t = sb("ln_t", [128, 2, D], F32, pc_)
            for i_ in range(2):
                setup_load('sp', ln_t[:, i_, :], lnv[i_:i_ + 1, :].partition_broadcast(128), R_ln)
            wsT = sb("wsT_s", [128, 1024], BF16, pc_)
            sbias = sb("sbias_s", [1, 1024], BF16, pc_)
            setup_load('pool', wsT[:], wsT_d[:, :], R_ws)
            setup_load('pool', sbias[:], sbias_d[:, :], R_sbias)
            ssum = sb("ssumC", [128, 4], F32, pc_)
            rstd = sb("rstdC", [128, 4], F32, pc_)
            R_tmp = Res()
            ftmp = [sb(f"ftmp{i}", [128, 512], F32, pc_) for i in range(4)]
            R_ft = [Res() for _ in range(4)]
            fst = {'i': 0}

            def wload(w_ap, r0, nk, c0, ncol):
                i = wst['i']
                wst['i'] = (i + 1) % NSLOT
                sl, Rsl, ch = wsl[i], R_ws_[i], c_ws[i]
                wb = BFW[w_ap.name][1]
                S.op('sp', lambda e: e.dma_start(
                    out=sl[:, 0:nk, 0:ncol],
                    in_=wb[r0 * 128:(r0 + nk) * 128, c0:c0 + ncol].rearrange("(k p) c -> p k c", p=128)),
                    r=[R_bfw], w=[Rsl], chan=ch)
                return sl, Rsl

            def nft():
                i = fst['i']
                fst['i'] = (i + 1) % 4
                return ftmp[i], R_ft[i]

            def norm_block(gi):
                for t_ in range(4):
                    nbb, Rnb = nbt[t_ % 2], R_nbt[t_ % 2]
                    rmsnorm_to(nbb[:], hb[t_][:], g4[:, gi, :], junk[:], ssum[:, 0:1], rstd[:, 0:1], [R_h[t_]], [Rnb], R_tmp)
                    transpose_to(nT, t_ * 128, nbb[:], [Rnb], [R_nT[t_]])

            def proj_fm(w_ap, c0, nmt, evac):
                sl, Rsl = wload(w_ap, 0, 8, c0, nmt * 128)
                for m_ in range(nmt):
                    bk, bR = nb()
                    for k in range(8):
                        mm(bk[:, :], sl[:, k, m_ * 128:(m_ + 1) * 128], nT[:, k, :], k == 0, k == 7, [Rsl] + R_nT, [bR])
                    evac(m_, bk, bR)

            def out_proj_tm(w_ap, nkt, srcT, srcR):
                for half in range(2):
                    chunks = []
                    r0 = 0
                    while r0 < nkt:
                        nk = min(8, nkt - r0)
                        chunks.append((r0, nk))
                        r0 += nk
                    bks = [nb() for _ in range(4)]
                    for (r0, nk) in chunks:
                        sl, Rsl = wload(w_ap, r0, nk, half * 512, 512)
                        for t_ in range(4):
                            bk, bR = bks[t_]
                            for k in range(nk):
                                kk = r0 + k
                                mm(bk[:, :], srcT(kk)[:, t_ * 128:(t_ + 1) * 128], sl[:, k, :], kk == 0, kk == nkt - 1,
                                   [Rsl] + srcR(kk), [bR])
                    for t_ in range(4):
                        bk, bR = bks[t_]
                        hv = hb[t_][:, half * 512:(half + 1) * 512]
                        tt('dve', hv, hv, bk[:, :], ALU.add, [bR, R_h[t_]], [R_h[t_]])

            sc = lambda s_: scr[:, s_, :, :]
            for blk in range(8):
                tok0 = blk * 512
                for t_ in range(4):
                    S.op('pool', lambda e, t_=t_, tok0=tok0: e.dma_start(out=hb[t_][:], in_=xl[tok0 + t_ * 128:tok0 + (t_ + 1) * 128, :]),
                         w=[R_h[t_]], chan=c_h[t_])
                norm_block(0)
                for part, (c0, func) in enumerate(((512, AF.Gelu_apprx_tanh), (2560, AF.Sigmoid), (3584, AF.Sigmoid))):
                    for half in range(2):
                        s_ = part * 2 + half

                        def ev(m_, bk, bR, s_=s_, func=func):
                            act(scr[:, s_, m_, :], bk[:, :], func, r=[bR], w=[R_scr[s_]])
                        proj_fm(w_in, c0 + half * 512, 4, ev)
                zsl = [wload(w_in, 0, 8, 1536 + half * 512, 512) for half in range(2)]
                for t_ in range(4):
                    zf, Rzf = zvf[t_ % 2], R_zvf[t_ % 2]
                    zn, Rzn = zvn[t_ % 2], R_zvn[t_ % 2]
                    for half in range(2):
                        sl, Rsl = zsl[half]
                        bk, bR = nb()
                        for k in range(8):
                            mm(bk[:, :], nT[:, k, t_ * 128:(t_ + 1) * 128], sl[:, k, :], k == 0, k == 7, [Rsl] + R_nT, [bR])
                        act(zf[:, half * 512:(half + 1) * 512], bk[:, :], AF.Gelu_apprx_tanh, r=[bR], w=[Rzf, R_tmp],
                            accum=ssum[:, half:half + 1])
                    act(junk[:], zf[:], AF.Square, r=[Rzf], w=[R_tmp], accum=ssum[:, 2:3])
                    tt('dve', ssum[:, 0:1], ssum[:, 0:1], ssum[:, 1:2], ALU.add, [R_tmp], [R_tmp])
                    ts('dve', rstd[:, 1:2], ssum[:, 0:1], 1.0 / D, None, ALU.mult, None, [R_tmp], [R_tmp])
                    tt('dve', rstd[:, 2:3], rstd[:, 1:2], rstd[:, 1:2], ALU.mult, [R_tmp], [R_tmp])
                    stt(rstd[:, 2:3], ssum[:, 2:3], 1.0 / D, rstd[:, 2:3], ALU.mult, ALU.subtract, [R_tmp], [R_tmp])
                    act(rstd[:, 2:3], rstd[:, 2:3], AF.Sqrt, r=[R_tmp], w=[R_tmp], bias=epsb[:, 0:1])
                    S.op('dve', lambda e: e.reciprocal(out=rstd[:, 2:3], in_=rstd[:, 2:3]), r=[R_tmp], w=[R_tmp])
                    ts('dve', zf[:], zf[:], rstd[:, 1:2], rstd[:, 2:3], ALU.subtract, ALU.mult, [R_tmp, Rzf], [Rzf])
                    tt('dve', zf[:], zf[:], ln_t[:, 0, :], ALU.mult, [Rzf, R_ln], [Rzf])
                    tt('dve', zn[:], zf[:], ln_t[:, 1, :], ALU.add, [Rzf, R_ln], [Rzn])
                    for h_ in range(8):
                        bk, bR = nb()
                        mm(bk[:, 0:128], zn[:, h_ * 128:(h_ + 1) * 128], wsT[:, h_ * 128:(h_ + 1) * 128], True, False, [Rzn, R_ws], [bR])
                        mm(bk[:, 0:128], ones_b[0:1, :], sbias[0:1, h_ * 128:(h_ + 1) * 128], False, True, [R_ident, R_sbias], [bR])
                        s_ = 6 + h_ // 4
                        zu_ = scr[:, h_ // 4, h_ % 4, t_ * 128:(t_ + 1) * 128]
                        tt('dve', scr[:, s_, h_ % 4, t_ * 128:(t_ + 1) * 128], bk[:, 0:128], zu_, ALU.mult,
                           [bR, R_scr[h_ // 4]], [R_scr[s_]])
                sl, Rsl = wload(w_glu, 0, 4, 0, 512)
                for m_ in range(4):
                    bk, bR = nb()
                    for k in range(4):
                        mm(bk[:, :], sl[:, k, m_ * 128:(m_ + 1) * 128], uT_own[:, k, tok0:tok0 + 512], k == 0, k == 3,
                           [Rsl, R_uo[k]], [bR])
                    ft, Rf = nft()
                    act(ft[:], bk[:, :], AF.Sigmoid, r=[bR], w=[Rf])
                    tt('dve', scr[:, 8, m_, :], ft[:], uT_own[:, m_, tok0:tok0 + 512], ALU.mult, [Rf, R_uo[m_]], [R_scr[8]])
                for half in range(2):
                    sla, Rsla = wload(w_pa, 0, 4, half * 512, 512)
                    slb, Rslb = wload(w_pb, 0, 8, half * 512, 512)
                    for m_ in range(4):
                        bka, bRa = nb()
                        for k in range(4):
                            mm(bka[:, :], sla[:, k, m_ * 128:(m_ + 1) * 128], scr[:, 8, k, :], k == 0, k == 3, [Rsla, R_scr[8]], [bRa])
                        bkb, bRb = nb()
                        for k in range(8):
                            mm(bkb[:, :], slb[:, k, m_ * 128:(m_ + 1) * 128], scr[:, 6 + k // 4, k % 4, :], k == 0, k == 7,
                               [Rslb, R_scr[6 + k // 4]], [bRb])
                        ft, Rf = nft()
                        tt('dve', ft[:], bka[:, :], scr[:, 2 + half, m_, :], ALU.mult, [bRa, R_scr[2 + half]], [Rf])
                        ft2, Rf2 = nft()
                        tt('dve', ft2[:], bkb[:, :], scr[:, 4 + half, m_, :], ALU.mult, [bRb, R_scr[4 + half]], [Rf2])
                        tt('pool', scr[:, 9 + half, m_, :], ft[:], ft2[:], ALU.add, [Rf, Rf2], [R_scr[9 + half]])
                out_proj_tm(w_out, 8, lambda kk: scr[:, 9 + kk // 4, kk % 4, :], lambda kk: [R_scr[9 + kk // 4]])
                norm_block(1)
                for half in range(2):
                    def evq(m_, bk, bR, half=half):
                        act(scr[:, half, m_, :], bk[:, :], AF.Copy, r=[bR], w=[R_scr[half]], scale=1.0 / 16.0)
                    proj_fm(w_q, half * 512, 4, evq)
                for hd in range(4):
                    qs = hd // 2
                    for mt in range(2):
                        bk, bR = nb()
                        for dt_ in range(2):
                            mm(bk[:, :], kT[:, hd * 2 + dt_, mt * 128:(mt + 1) * 128], scr[:, qs, (hd % 2) * 2 + dt_, :],
                               dt_ == 0, dt_ == 1, [R_kT, R_scr[qs]], [bR])
                        act(scr[:, 2, mt, :], bk[:, :], AF.Exp, r=[bR], w=[R_scr[2]])
                    bk, bR = nb()
                    for mt in range(2):
                        mm(bk[:, :], ones_b[:], scr[:, 2, mt, :], mt == 0, mt == 1, [R_ident, R_scr[2]], [bR])
                    ft, Rf = nft()
                    S.op('dve', lambda e, ft=ft, bk=bk: e.reciprocal(out=ft[:], in_=bk[:, :]), r=[bR], w=[Rf])
                    for dt_ in range(2):
                        bk2, bR2 = nb()
                        for mt in range(2):
                            mm(bk2[:, :], vtok[:, mt, (hd * 2 + dt_) * 128:(hd * 2 + dt_ + 1) * 128], scr[:, 2, mt, :],
                               mt == 0, mt == 1, [R_v, R_scr[2]], [bR2])
                        kk = hd * 2 + dt_
                        tt('dve', scr[:, 3 + kk // 4, kk % 4, :], bk2[:, :], ft[:], ALU.mult, [bR2, Rf], [R_scr[3 + kk // 4]])
                out_proj_tm(w_xo, 8, lambda kk: scr[:, 3 + kk // 4, kk % 4, :], lambda kk: [R_scr[3 + kk // 4]])
                norm_block(2)
                c0 = 0
                while c0 < DFF:
                    ncol = min(512, DFF - c0)
                    nmt = ncol // 128
                    slg, Rg = wload(w_gate, 0, 8, c0, ncol)
                    slu, Ru = wload(w_up, 0, 8, c0, ncol)
                    for m_ in range(nmt):
                        mi = c0 // 128 + m_
                        bkg, bRg = nb()
                        for k in range(8):
                            mm(bkg[:, :], slg[:, k, m_ * 128:(m_ + 1) * 128], nT[:, k, :], k == 0, k == 7, [Rg] + R_nT, [bRg])
                        bku, bRu = nb()
                        for k in range(8):
                            mm(bku[:, :], slu[:, k, m_ * 128:(m_ + 1) * 128], nT[:, k, :], k == 0, k == 7, [Ru] + R_nT, [bRu])
                        ft, Rf = nft()
                        act(ft[:], bkg[:, :], AF.Silu, r=[bRg], w=[Rf])
                        tt('dve', scr[:, mi // 4, mi % 4, :], ft[:], bku[:, :], ALU.mult, [Rf, bRu], [R_scr[mi // 4]])
                    c0 += ncol
                out_proj_tm(w_down, 22, lambda kk: scr[:, kk // 4, kk % 4, :], lambda kk: [R_scr[kk // 4]])
                last = None
                for t_ in range(4):
                    act(junk[:], hb[t_][:], AF.Square, r=[R_h[t_]], w=[R_tmp], accum=ssum[:, 0:1])
                    act(rstd[:, 0:1], ssum[:, 0:1], AF.Sqrt, r=[R_tmp], w=[R_tmp], scale=1.0 / D, bias=epsb[:, 0:1])
                    S.op('dve', lambda e: e.reciprocal(out=rstd[:, 0:1], in_=rstd[:, 0:1]), r=[R_tmp], w=[R_tmp])
                    stt(hb[t_][:], hb[t_][:], rstd[:, 0:1], g4[:, 3, :], ALU.mult, ALU.mult, [R_h[t_], R_tmp, R_gb], [R_h[t_]])
                    last = S.op('pool', lambda e, t_=t_, tok0=tok0: e.dma_start(
                        out=out_d[tok0 + t_ * 128:tok0 + (t_ + 1) * 128, :], in_=hb[t_][:]), r=[R_h[t_]], chan=c_o[t_])
                    finals.append(last)
            S.barrier()
        S.emit(final_waits=finals)
    except _Stop:
        pass
    return nc


finals = []

_CACHE = {}


def _host_layout(inputs, core):
    b, hf = core // 2, core % 2
    rev = hf == 1
    f = lambda a: np.ascontiguousarray(a, dtype=np.float32)
    x = inputs["x"][b]
    xl = x[::-1] if rev else x
    dsel = [1, 0] if rev else [0, 1]
    lre = inputs["s5_lam_re"][0][dsel]
    lim = inputs["s5_lam_im"][0][dsel]
    ls = inputs["s5_log_step"][0][dsel]
    bre = inputs["s5_b_re"][0][dsel]
    bim = inputs["s5_b_im"][0][dsel]
    cre = inputs["s5_c_re"][0][dsel]
    cim = inputs["s5_c_im"][0][dsel]
    def layA_gp(a):
        t = a.reshape(2, 4, 8, 64)
        t = np.broadcast_to(t[:, :, :, None, :], (2, 4, 8, 16, 64))
        return t.transpose(2, 3, 0, 1, 4).reshape(128, 512)
    lsA = np.broadcast_to(ls[:, :, None], (2, 32, 64))
    def layA_b(a):
        t = a.reshape(2, 4, 8, 64, 16)
        return t.transpose(2, 4, 0, 1, 3).reshape(128, 512)
    A_in = np.stack([layA_gp(lre), layA_gp(lim), layA_gp(lsA), layA_b(bre), layA_b(bim)])
    def layB_gp(a):
        t = a.reshape(2, 16, 2, 64)
        t = np.broadcast_to(t[:, :, :, :, None], (2, 16, 2, 64, 16))
        return t.transpose(2, 3, 0, 1, 4).reshape(128, 512)
    def layB_c(a):
        t = a.reshape(2, 16, 2, 16, 64)
        return t.transpose(2, 4, 0, 1, 3).reshape(128, 512)
    B_in = np.stack([layB_gp(lre), layB_gp(lim), layB_gp(lsA), layB_c(cre), layB_c(cim)])
    Cc = np.concatenate([cre.transpose(3, 0, 1, 2).reshape(64, 1024), cim.transpose(3, 0, 1, 2).reshape(64, 1024)], axis=0)
    dcol = inputs["s5_d"][0].reshape(4, 128).T
    bmask = np.kron(np.eye(8, dtype=np.float32), np.ones((16, 16), np.float32))
    pm = ((np.arange(128) // 16) % 2)
    pmask = np.stack([(pm == 0), (pm == 1)], axis=1).astype(np.float32)
    ws = inputs["sgu_w"][0]
    sbias = inputs["sgu_bias"][0]
    if rev:
        ws = ws[:, ::-1, ::-1]
        sbias = sbias[:, ::-1]
    wsT = ws.transpose(2, 0, 1).reshape(128, 1024)
    gvec = np.stack([inputs["mix_norm_g"][0], inputs["xattn_norm_g"][0], inputs["mem_norm_g"],
                     inputs["ffn_norm_g"][0], inputs["final_norm_g"]])
    lnv = np.stack([inputs["sgu_ln_g"][0], inputs["sgu_ln_b"][0]])
    m = {
        "xl": f(xl), "mem": f(inputs["mem"][b]),
        "w_in": f(inputs["w_in"][0]), "w_glu": f(inputs["s5_w_glu"][0]),
        "w_pa": f(inputs["w_proj_a"][0]), "w_pb": f(inputs["w_proj_b"][0]), "w_out": f(inputs["w_out"][0]),
        "w_q": f(inputs["w_q"][0]), "w_k": f(inputs["w_k"][0]), "w_v": f(inputs["w_v"][0]), "w_xo": f(inputs["w_xo"][0]),
        "w_gate": f(inputs["w_gate"][0]), "w_up": f(inputs["w_up"][0]), "w_down": f(inputs["w_down"][0]),
        "gvec": f(gvec), "lnv": f(lnv), "wsT": f(wsT), "sbias": f(sbias.reshape(1, 1024)),
        "A_in": f(A_in), "B_in": f(B_in), "Cc": f(Cc), "dcol": f(dcol), "bmask": f(bmask), "pmask": f(pmask),
    }
    return m


def kernel(**inputs):
    inputs = {k: np.asarray(v) for k, v in inputs.items()}
    if "nc" not in _CACHE:
        finals.clear()
        _CACHE["nc"] = build_program()
    nc = _CACHE["nc"]
    in_maps = [_host_layout(inputs, c) for c in range(8)]
    res = run_bass_kernel_spmd(nc, in_maps, core_ids=list(range(8)))
    out = np.empty((4, 8192, D), np.float32)
    for c in range(8):
        b, hf = c // 2, c % 2
        o = np.asarray(res.results[c]["out"])
        if hf == 0:
            out[b, 0:4096] = o
        else:
            out[b, 4096:8192] = o[::-1]
    return out
```
